# Optimizing a Trainium2 kernel written in Bass

```python
import math
import jax
import jax.numpy as jnp
from jax import lax
import numpy as np

D_MODEL = 1024
BATCH = 8
SEQ = 2048
DEPTH = 1
DEC_BATCH = 128
DEC_SEQ = 8
PAST_LEN = 16384
PAGE_SIZE = 128

MIX_WIDTH = 2 * D_MODEL
SSD_WIDTH = MIX_WIDTH // 2
RWKV_WIDTH = MIX_WIDTH - SSD_WIDTH
SSD_HEAD_DIM = 64
SSD_HEADS = SSD_WIDTH // SSD_HEAD_DIM
SSD_GROUPS = 2
SSD_HPG = SSD_HEADS // SSD_GROUPS
SSD_STATE = 128
SSD_CONV = 4
SSD_CHUNK = 128
SSD_CONV_DIM = SSD_WIDTH + 2 * SSD_GROUPS * SSD_STATE
RWKV_HEAD_DIM = 64
RWKV_HEADS = RWKV_WIDTH // RWKV_HEAD_DIM
DECAY_LORA = 64
AAA_LORA = 64
GATE_LORA = 128
RWKV_PROJ = 3 * RWKV_WIDTH + DECAY_LORA + AAA_LORA + GATE_LORA
IN_PROJ = SSD_WIDTH + SSD_CONV_DIM + SSD_HEADS + RWKV_PROJ
IN_SPLITS = (SSD_WIDTH, SSD_WIDTH + SSD_CONV_DIM, SSD_WIDTH + SSD_CONV_DIM + SSD_HEADS)
XBC_SPLITS = (SSD_WIDTH, SSD_WIDTH + SSD_GROUPS * SSD_STATE)
RWKV_SPLITS = (RWKV_WIDTH, 2 * RWKV_WIDTH, 3 * RWKV_WIDTH, 3 * RWKV_WIDTH + DECAY_LORA,
               3 * RWKV_WIDTH + DECAY_LORA + AAA_LORA)
D_FF = -(-8 * D_MODEL // (3 * 256)) * 256
PLE_DIM = 256
NORM_EPS = 1e-6
GN_EPS = 64e-5

kernel_name = 'hymba_ssd_rwkv7_ple_step'


def rmsnorm(x, g):
    xf = x.astype(jnp.float32)
    y = xf * lax.rsqrt(jnp.mean(xf * xf, axis=-1, keepdims=True) + NORM_EPS)
    return (y * g.astype(jnp.float32)).astype(x.dtype)


def ssd_chunked(x, dt, a_head, bm, cm, h0):
    b, l, g, e, p = x.shape
    n = bm.shape[-1]
    q = SSD_CHUNK if l % SSD_CHUNK == 0 else l
    c = l // q
    f32 = jnp.float32
    x = x.astype(f32).reshape(b, c, q, g, e, p)
    dt = dt.astype(f32).reshape(b, c, q, g, e)
    bm = bm.astype(f32).reshape(b, c, q, g, n)
    cm = cm.astype(f32).reshape(b, c, q, g, n)
    a_cum = jnp.cumsum(dt * a_head.astype(f32), axis=2)
    seg = a_cum[:, :, :, None] - a_cum[:, :, None, :]
    causal = jnp.tril(jnp.ones((q, q), dtype=bool))[None, None, :, :, None, None]
    decay_qs = jnp.exp(jnp.where(causal, seg, -jnp.inf))
    cb = jnp.einsum('bcqgn,bcsgn->bcqsg', cm, bm)
    w_qs = cb[..., None] * decay_qs * dt[:, :, None]
    y_diag = jnp.einsum('bcqsge,bcsgep->bcqgep', w_qs, x)
    decay_end = jnp.exp(a_cum[:, :, -1:] - a_cum) * dt
    chunk_states = jnp.einsum('bcsgn,bcsge,bcsgep->bcgepn', bm, decay_end, x)
    chunk_decay = jnp.exp(a_cum[:, :, -1])

    def step(h, inp):
        s_c, d_c = inp
        return h * d_c[..., None, None] + s_c, h

    h_fin, h_in = lax.scan(step, h0.astype(f32),
                           (jnp.moveaxis(chunk_states, 1, 0), jnp.moveaxis(chunk_decay, 1, 0)))
    h_in = jnp.moveaxis(h_in, 0, 1)
    y_off = jnp.einsum('bcqgn,bcgepn->bcqgep', cm, h_in) * jnp.exp(a_cum)[..., None]
    return (y_diag + y_off).reshape(b, l, g, e, p), h_fin


def wkv_scan(r, decay, k, v, kk, a, s0):
    f32 = jnp.float32
    seq = tuple(jnp.moveaxis(t.astype(f32), 1, 0) for t in (r, decay, k, v, kk, a))

    def step(s, inp):
        r_t, w_t, k_t, v_t, kk_t, a_t = inp
        s_kk = jnp.einsum('bhvk,bhk->bhv', s, kk_t)
        s = (s * w_t[:, :, None, :] - s_kk[..., None] * (kk_t * a_t)[:, :, None, :]
             + v_t[..., None] * k_t[:, :, None, :])
        return s, jnp.einsum('bhvk,bhk->bhv', s, r_t)

    s_fin, out = lax.scan(step, s0.astype(f32), seq)
    return jnp.moveaxis(out, 0, 1), s_fin


def token_mixers(hn, conv_buf, shift_buf, ssm0, wkv0, w):
    b, l, _ = hn.shape
    dtype = hn.dtype
    f32 = jnp.float32
    proj = hn @ w['w_in']
    z, xbc, dt_raw, rw = jnp.split(proj, IN_SPLITS, axis=-1)

    xbc_full = jnp.concatenate([conv_buf.astype(xbc.dtype), xbc], axis=1)
    conv = lax.conv_general_dilated(
        xbc_full, w['conv_w'][:, None, :].astype(xbc.dtype), (1,), 'VALID',
        dimension_numbers=('NWC', 'WIO', 'NWC'), feature_group_count=SSD_CONV_DIM) + w['conv_b'].astype(xbc.dtype)
    conv_new = xbc_full[:, -(SSD_CONV - 1):]
    xbc_act = jax.nn.silu(conv)
    xs, bm, cm = jnp.split(xbc_act, XBC_SPLITS, axis=-1)
    xs = xs.reshape(b, l, SSD_GROUPS, SSD_HPG, SSD_HEAD_DIM)
    bm = bm.reshape(b, l, SSD_GROUPS, SSD_STATE)
    cm = cm.reshape(b, l, SSD_GROUPS, SSD_STATE)
    dt = jax.nn.softplus(dt_raw.astype(f32) + w['dt_bias'].astype(f32)).reshape(b, l, SSD_GROUPS, SSD_HPG)
    a_head = -jnp.exp(w['a_log'].astype(f32)).reshape(SSD_GROUPS, SSD_HPG)
    y, ssm_fin = ssd_chunked(xs, dt, a_head, bm, cm,
                             ssm0.reshape(b, SSD_GROUPS, SSD_HPG, SSD_HEAD_DIM, SSD_STATE))
    y = y + w['d_skip'].astype(f32).reshape(SSD_GROUPS, SSD_HPG, 1) * xs.astype(f32)
    yg = (y.reshape(b, l, SSD_WIDTH) * jax.nn.silu(z.astype(f32))).reshape(b, l, SSD_GROUPS, SSD_WIDTH // SSD_GROUPS)
    yg = yg * lax.rsqrt(jnp.mean(yg * yg, axis=-1, keepdims=True) + NORM_EPS)
    y_ssd = yg.reshape(b, l, SSD_WIDTH) * w['ssd_norm'].astype(f32)

    prev = jnp.concatenate([shift_buf.astype(rw.dtype), rw], axis=1)[:, :-1]
    shift_new = rw[:, -1:]
    u = rw + (prev - rw) * w['shift_mu']
    r, k, v, w_lo, a_lo, g_lo = jnp.split(u, RWKV_SPLITS, axis=-1)
    w_log = -jax.nn.softplus(-(w['w0'] + jnp.tanh(w_lo) @ w['w2']).astype(f32)) - 0.5
    decay = jnp.exp(-jnp.exp(w_log))
    a = jax.nn.sigmoid((w['a0'] + a_lo @ w['a2']).astype(f32))
    gate = (jax.nn.sigmoid(g_lo) @ w['g2']).astype(f32)

    def heads(t):
        return t.reshape(b, l, RWKV_HEADS, RWKV_HEAD_DIM)

    kk = heads((k * w['k_k']).astype(f32))
    kk = kk / jnp.maximum(jnp.sqrt(jnp.sum(kk * kk, axis=-1, keepdims=True)), 1e-12)
    k = k.astype(f32) * (1.0 + (a - 1.0) * w['k_a'].astype(f32))
    rh, kh, vh = heads(r.astype(f32)), heads(k), heads(v.astype(f32))
    o, wkv_fin = wkv_scan(rh, heads(decay), kh, vh, kk, heads(a), wkv0)
    mu = jnp.mean(o, axis=-1, keepdims=True)
    var = jnp.mean(jnp.square(o - mu), axis=-1, keepdims=True)
    on = ((o - mu) * lax.rsqrt(var + GN_EPS)).reshape(b, l, RWKV_WIDTH)
    on = on * w['ln_x_w'].astype(f32) + w['ln_x_b'].astype(f32)
    r_k = w['r_k'].astype(f32).reshape(RWKV_HEADS, RWKV_HEAD_DIM)
    bonus = jnp.sum(rh * kh * r_k, axis=-1, keepdims=True) * vh
    y_rwkv = (on + bonus.reshape(b, l, RWKV_WIDTH)) * gate

    out = jnp.concatenate([y_ssd, y_rwkv], axis=-1).astype(dtype) @ w['w_out']
    ssm_new = ssm_fin.reshape(b, SSD_HEADS, SSD_HEAD_DIM, SSD_STATE).astype(dtype)
    return out, conv_new.astype(dtype), shift_new.astype(dtype), ssm_new, wkv_fin.astype(dtype)


def layer(h, p, conv_buf, shift_buf, ssm0, wkv0, w):
    mix, conv_new, shift_new, ssm_new, wkv_new = token_mixers(
        rmsnorm(h, w['norm_mix']), conv_buf, shift_buf, ssm0, wkv0, w)
    h = h + mix
    hf = rmsnorm(h, w['norm_ffn'])
    h = h + (jax.nn.silu(hf @ w['w_gate']) * (hf @ w['w_up'])) @ w['w_down']
    gate = jax.nn.sigmoid(rmsnorm(h, w['norm_ple']) @ w['w_ple_gate'])
    h = h + gate * (p.astype(h.dtype) @ w['w_ple_proj'])
    return h, ssm_new, conv_new, wkv_new, shift_new


def setup_inputs(seed: int = 0) -> dict:
    key = jax.random.key(seed)
    ks = list(jax.random.split(key, 40))

    def nrm(shape, scale):
        return scale * jax.random.normal(ks.pop(), shape, jnp.float32)

    def unif(shape, lo, hi):
        return jax.random.uniform(ks.pop(), shape, jnp.float32, lo, hi)

    L = (DEPTH,)
    dt0 = jnp.exp(unif(L + (SSD_HEADS,), math.log(1e-3), math.log(1e-1)))
    return {
        'x_prompt': nrm((BATCH, SEQ, D_MODEL), 1.0),
        'x_sample': nrm((DEC_BATCH, DEC_SEQ, D_MODEL), 1.0),
        'state_ssm': nrm(L + (DEC_BATCH, SSD_HEADS, SSD_HEAD_DIM, SSD_STATE), 0.1),
        'state_conv': nrm(L + (DEC_BATCH, SSD_CONV - 1, SSD_CONV_DIM), 1.0),
        'state_wkv': nrm(L + (DEC_BATCH, RWKV_HEADS, RWKV_HEAD_DIM, RWKV_HEAD_DIM), 0.1),
        'state_shift': nrm(L + (DEC_BATCH, 1, RWKV_PROJ), 1.0),
        'p_prompt': nrm((DEPTH, BATCH, SEQ, PLE_DIM), 1.0),
        'p_sample': nrm((DEPTH, DEC_BATCH, DEC_SEQ, PLE_DIM), 1.0),
        'norm_mix': 1.0 + nrm(L + (D_MODEL,), 0.05),
        'w_in': nrm(L + (D_MODEL, IN_PROJ), D_MODEL ** -0.5),
        'conv_w': nrm(L + (SSD_CONV, SSD_CONV_DIM), SSD_CONV ** -0.5),
        'conv_b': nrm(L + (SSD_CONV_DIM,), 0.01),
        'dt_bias': dt0 + jnp.log(-jnp.expm1(-dt0)),
        'a_log': jnp.log(unif(L + (SSD_HEADS,), 1.0, 16.0)),
        'd_skip': 1.0 + nrm(L + (SSD_HEADS,), 0.1),
        'ssd_norm': 1.0 + nrm(L + (SSD_WIDTH,), 0.05),
        'shift_mu': unif(L + (RWKV_PROJ,), 0.0, 1.0),
        'w0': -2.5 + nrm(L + (RWKV_WIDTH,), 0.5),
        'w2': nrm(L + (DECAY_LORA, RWKV_WIDTH), 0.1 * DECAY_LORA ** -0.5),
        'a0': nrm(L + (RWKV_WIDTH,), 0.1),
        'a2': nrm(L + (AAA_LORA, RWKV_WIDTH), AAA_LORA ** -0.5),
        'g2': nrm(L + (GATE_LORA, RWKV_WIDTH), GATE_LORA ** -0.5),
        'k_k': 0.85 + nrm(L + (RWKV_WIDTH,), 0.05),
        'k_a': 1.0 + nrm(L + (RWKV_WIDTH,), 0.05),
        'r_k': nrm(L + (RWKV_WIDTH,), 0.1),
        'ln_x_w': 1.0 + nrm(L + (RWKV_WIDTH,), 0.05),
        'ln_x_b': nrm(L + (RWKV_WIDTH,), 0.01),
        'w_out': nrm(L + (MIX_WIDTH, D_MODEL), MIX_WIDTH ** -0.5),
        'norm_ffn': 1.0 + nrm(L + (D_MODEL,), 0.05),
        'w_gate': nrm(L + (D_MODEL, D_FF), D_MODEL ** -0.5),
        'w_up': nrm(L + (D_MODEL, D_FF), D_MODEL ** -0.5),
        'w_down': nrm(L + (D_FF, D_MODEL), D_FF ** -0.5),
        'norm_ple': 1.0 + nrm(L + (D_MODEL,), 0.05),
        'w_ple_gate': nrm(L + (D_MODEL, D_MODEL), D_MODEL ** -0.5),
        'w_ple_proj': nrm(L + (PLE_DIM, D_MODEL), PLE_DIM ** -0.5),
        'norm_final': 1.0 + nrm((D_MODEL,), 0.05),
    }


def reference(x_prompt, x_sample, state_ssm, state_conv, state_wkv, state_shift, p_prompt, p_sample,
              norm_mix, w_in, conv_w, conv_b, dt_bias, a_log, d_skip, ssd_norm, shift_mu, w0, w2, a0, a2,
              g2, k_k, k_a, r_k, ln_x_w, ln_x_b, w_out, norm_ffn, w_gate, w_up, w_down, norm_ple,
              w_ple_gate, w_ple_proj, norm_final):
    bp = x_prompt.shape[0]
    dtype = x_prompt.dtype
    hp, hs = x_prompt, x_sample
    ssm_p, conv_p, wkv_p, shift_p = [], [], [], []
    ssm_s, conv_s, wkv_s, shift_s = [], [], [], []
    for i in range(DEPTH):
        w = {
            'norm_mix': norm_mix[i], 'w_in': w_in[i], 'conv_w': conv_w[i], 'conv_b': conv_b[i],
            'dt_bias': dt_bias[i], 'a_log': a_log[i], 'd_skip': d_skip[i], 'ssd_norm': ssd_norm[i],
            'shift_mu': shift_mu[i], 'w0': w0[i], 'w2': w2[i], 'a0': a0[i], 'a2': a2[i], 'g2': g2[i],
            'k_k': k_k[i], 'k_a': k_a[i], 'r_k': r_k[i], 'ln_x_w': ln_x_w[i], 'ln_x_b': ln_x_b[i],
            'w_out': w_out[i], 'norm_ffn': norm_ffn[i], 'w_gate': w_gate[i], 'w_up': w_up[i],
            'w_down': w_down[i], 'norm_ple': norm_ple[i], 'w_ple_gate': w_ple_gate[i],
            'w_ple_proj': w_ple_proj[i],
        }
        hp, s1, c1, k1, t1 = layer(
            hp, p_prompt[i],
            jnp.zeros((bp, SSD_CONV - 1, SSD_CONV_DIM), dtype),
            jnp.zeros((bp, 1, RWKV_PROJ), dtype),
            jnp.zeros((bp, SSD_HEADS, SSD_HEAD_DIM, SSD_STATE), dtype),
            jnp.zeros((bp, RWKV_HEADS, RWKV_HEAD_DIM, RWKV_HEAD_DIM), dtype), w)
        ssm_p.append(s1); conv_p.append(c1); wkv_p.append(k1); shift_p.append(t1)
        hs, s2, c2, k2, t2 = layer(hs, p_sample[i], state_conv[i], state_shift[i], state_ssm[i], state_wkv[i], w)
        ssm_s.append(s2); conv_s.append(c2); wkv_s.append(k2); shift_s.append(t2)
    y_prompt = rmsnorm(hp, norm_final)
    y_sample = rmsnorm(hs, norm_final)
    return (y_prompt, y_sample,
            jnp.stack(ssm_p), jnp.stack(conv_p), jnp.stack(wkv_p), jnp.stack(shift_p),
            jnp.stack(ssm_s), jnp.stack(conv_s), jnp.stack(wkv_s), jnp.stack(shift_s))
```

```python
import numpy as np
import concourse.bass as bass
import concourse.mybir as mybir
from concourse.bass_utils import run_bass_kernel_spmd
from contextlib import ExitStack

F32 = mybir.dt.float32
BF16 = mybir.dt.bfloat16
AF = mybir.ActivationFunctionType
ALU = mybir.AluOpType
AX = mybir.AxisListType

NCORES = 8
D = 1024
INP = 5904
DFF = 2816
C_Z, C_XBC, C_DT, C_RW = 0, 1024, 2560, 2576
CEXP = float(np.exp(-0.5))
NORM_EPS = 1e-6
GN_EPS = 64e-5
NTOK = 2048 + 128


class MK:
    ENG = ("pe", "act", "dve", "pool", "sp")

    def __init__(self, nc, es):
        self.nc = nc
        self.es = es
        self.ops = {e: [] for e in self.ENG}
        self.sem = {e: es.enter_context(nc.semaphore("s_" + e)) for e in self.ENG}
        self.cnt = {id(s): 0 for s in self.sem.values()}
        self.dsem = {}
        self.res = {}
        self.waited = {e: {} for e in self.ENG}
        self.allsems = {id(s): s for s in self.sem.values()}
        self.ninst = 0

    def sb(self, name, shape, dt=F32):
        return self.es.enter_context(self.nc.sbuf_tensor(name, list(shape), dt))

    def ps(self, name, shape, dt=F32):
        return self.es.enter_context(self.nc.psum_tensor(name, list(shape), dt))

    def _dma_sem(self, key):
        s = self.dsem.get(key)
        if s is None:
            s = self.es.enter_context(self.nc.semaphore("d%d" % len(self.dsem)))
            self.dsem[key] = s
            self.cnt[id(s)] = 0
            self.allsems[id(s)] = s
        return s

    def _deps(self, eng, reads, writes):
        w = {}

        def need(p):
            if p is None:
                return
            s, v = p
            if w.get(s, 0) < v:
                w[s] = v

        for r in reads:
            st = self.res.get(r)
            if st:
                need(st[0])
        for k in writes:
            st = self.res.get(k)
            if st:
                need(st[0])
                for s, v in st[1].items():
                    need((s, v))
        out = []
        for s, v in w.items():
            if self.waited[eng].get(s, 0) >= v:
                continue
            self.waited[eng][s] = v
            out.append((self.allsems[s], v))
        return out

    def _commit(self, reads, writes, sem, val):
        for r in reads:
            st = self.res.setdefault(r, [None, {}])
            st[1][id(sem)] = val
        for k in writes:
            self.res[k] = [(id(sem), val), {}]

    def op(self, eng, fn, reads=(), writes=()):
        waits = self._deps(eng, reads, writes)
        sem = self.sem[eng]
        val = self.cnt[id(sem)] + 1
        self.cnt[id(sem)] = val
        self._commit(reads, writes, sem, val)

        def emit(h, waits=waits, fn=fn, sem=sem):
            for s, v in waits:
                h.wait_ge(s, v)
            ins = fn(h)
            ins.then_inc(sem, 1)

        self.ops[eng].append(emit)
        self.ninst += 1

    def dma(self, eng, out_ap, in_ap, reads=(), writes=(), key=None, **kw):
        if key is None:
            key = writes[0] if writes else reads[0]
        sem = self._dma_sem(key)
        waits = self._deps(eng, reads, writes)
        val = self.cnt[id(sem)] + 16
        self.cnt[id(sem)] = val
        self._commit(reads, writes, sem, val)

        def emit(h, waits=waits, sem=sem):
            for s, v in waits:
                h.wait_ge(s, v)
            h.dma_start(out=out_ap, in_=in_ap, **kw).then_inc(sem, 16)

        self.ops[eng].append(emit)

    def barrier(self):
        finals = [(i, s, self.cnt[i]) for i, s in self.allsems.items() if self.cnt[i] > 0]
        for e in self.ENG:
            todo = []
            for i, s, v in finals:
                if self.waited[e].get(i, 0) < v:
                    self.waited[e][i] = v
                    todo.append((s, v))

            def emit(h, todo=todo):
                for s, v in todo:
                    h.wait_ge(s, v)

            self.ops[e].append(emit)
        self.res = {}

    def emit_all(self):
        nc = self.nc
        with nc.Block() as block:
            @block.tensor
            def _(h):
                for f in self.ops["pe"]:
                    f(h)

            @block.scalar
            def _(h):
                for f in self.ops["act"]:
                    f(h)

            @block.vector
            def _(h):
                for f in self.ops["dve"]:
                    f(h)

            @block.gpsimd
            def _(h):
                for f in self.ops["pool"]:
                    f(h)

            @block.sync
            def _(h):
                for f in self.ops["sp"]:
                    f(h)


def _prod(s):
    n = 1
    for x in s:
        n *= x
    return n


def _nd(ap, shape):
    if len(shape) == 1:
        return ap
    names = ["d%d" % i for i in range(len(shape))]
    kw = {names[i]: shape[i] for i in range(1, len(shape))}
    return ap.rearrange("p (%s) -> p %s" % (" ".join(names), " ".join(names)), **kw)


class Arena:
    def __init__(self, mk, name, words):
        self.t32 = mk.sb(name, [128, words], F32)
        self.t16 = self.t32.bitcast(BF16)
        self.words = words
        self.off = 0
        self.peak = 0

    def reset(self):
        self.off = 0

    def alloc(self, shape, dt=F32):
        n = _prod(shape)
        if dt == F32:
            ap = self.t32[:, self.off:self.off + n]
            self.off += n
        else:
            ap = self.t16[:, 2 * self.off:2 * self.off + n]
            self.off += (n + 1) // 2
        assert self.off <= self.words, ("arena overflow", self.off, self.words)
        self.peak = max(self.peak, self.off)
        return _nd(ap, list(shape))


def build(cfg=None):
    cfg = cfg or {}
    segs_cfg = cfg.get("segs", ["P0", "P1", "P2", "P3", "S"])
    dbg = cfg.get("dbg", ())

    nc = bass.Bass("TRN2", target_bir_lowering=False)

    def din(name, shape):
        return nc.dram_tensor(name, list(shape), F32, kind="ExternalInput")

    def dout(name, shape):
        return nc.dram_tensor(name, list(shape), F32, kind="ExternalOutput")

    xcat = din("xcat", [NTOK, D])
    pcat = din("pcat", [NTOK, 256])
    st_ssm = din("st_ssm", [16, 16, 64, 128])
    st_conv = din("st_conv", [48, 1536])
    st_wkv = din("st_wkv", [16, 16, 64, 64])
    st_shift = din("st_shift", [16, 3328])
    w_in = din("w_in", [D, INP])
    w_out = din("w_out", [2048, D])
    w_gate = din("w_gate", [D, DFF])
    w_up = din("w_up", [D, DFF])
    w_down = din("w_down", [DFF, D])
    w_plg = din("w_ple_gate", [D, D])
    w_plp = din("w_ple_proj", [256, D])
    w2 = din("w2", [64, D])
    a2 = din("a2", [64, D])
    g2 = din("g2", [128, D])
    vec = {}
    for nm, n in [("norm_mix", D), ("norm_ffn", D), ("norm_ple", D), ("norm_final", D), ("ssd_norm", D),
                  ("conv_b", 1536), ("shift_mu", 3328), ("w0", D), ("a0", D), ("k_k", D), ("k_a", D), ("r_k", D),
                  ("ln_x_w", D), ("ln_x_b", D), ("dt_bias", 16), ("a_log", 16), ("d_skip", 16)]:
        vec[nm] = din(nm, [1, n])
    conv_w = din("conv_w", [4, 1536])

    ycat = dout("ycat", [NTOK, D])
    o_ssm_p = dout("ssm_p", [1024, 128])
    o_conv_p = dout("conv_p", [3, 1536])
    o_wkv_p = dout("wkv_p", [16, 64, 64])
    o_shift_p = dout("shift_p", [26, 128])
    o_ssm_s = dout("ssm_s", [16, 16, 64, 128])
    o_conv_s = dout("conv_s", [48, 1536])
    o_wkv_s = dout("wkv_s", [16, 16, 64, 64])
    o_shift_s = dout("shift_s", [16, 3328])
    dbg_out = {}

    with ExitStack() as es:
        mk = MK(nc, es)
        PSh = mk.ps("PS", [128, 8, 512], F32)
        PS = PSh
        PS16 = PSh.bitcast(BF16)

        def TT(E, out, in0, in1, op, r, w):
            mk.op(E, lambda h: h.tensor_tensor(out=out, in0=in0, in1=in1, op=op), r, w)

        def TS(E, out, in0, s1, op0, r, w, s2=None, op1=None):
            if op1 is None:
                mk.op(E, lambda h: h.tensor_scalar(out=out, in0=in0, scalar1=s1, scalar2=None, op0=op0), r, w)
            else:
                mk.op(E, lambda h: h.tensor_scalar(out=out, in0=in0, scalar1=s1, scalar2=s2, op0=op0, op1=op1), r, w)

        def STT(E, out, in0, sc, in1, op0, op1, r, w):
            mk.op(E, lambda h: h.scalar_tensor_tensor(out=out, in0=in0, scalar=sc, in1=in1, op0=op0, op1=op1), r, w)

        def ACT(out, in_, func, r, w, bias=None, scale=None, accum=None):
            kw = {}
            if bias is not None:
                kw["bias"] = bias
            if scale is not None:
                kw["scale"] = scale
            if accum is not None:
                kw["accum_out"] = accum
            mk.op("act", lambda h: h.activation(out=out, in_=in_, func=func, **kw), r, w)

        def CP(E, out, in_, r, w):
            if E == "act":
                mk.op(E, lambda h: h.copy(out=out, in_=in_), r, w)
            else:
                mk.op(E, lambda h: h.tensor_copy(out=out, in_=in_), r, w)

        def MM(items, r, w):
            groups, cur, last_bp = [], [], None
            for it_ in items:
                bp = it_[1].base_partition()
                if cur and bp != last_bp:
                    groups.append(cur)
                    cur = []
                cur.append(it_)
                last_bp = bp
            groups.append(cur)
            for gi in groups:
                def f(h, gi=gi):
                    ins = None
                    for (o, l, rr, st, sp) in gi:
                        ins = h.matmul(o, lhsT=l, rhs=rr, start=st, stop=sp)
                    return ins
                mk.op("pe", f, r, w)

        def TR(items, ident, r, w):
            def f(h):
                ins = None
                for (o, i) in items:
                    ins = h.transpose(out=o, in_=i, identity=ident)
                return ins
            mk.op("pe", f, r, w)

        def SCAN(out, d0, d1, r, w):
            mk.op("dve", lambda h: h.tensor_tensor_scan(out=out, data0=d0, data1=d1, initial=0.0, op0=ALU.mult, op1=ALU.add), r, w)

        def RED(E, out, in_, r, w):
            mk.op(E, lambda h: h.tensor_reduce(out=out, in_=in_, axis=AX.X, op=ALU.add), r, w)

        def MEMSET(E, ap, val, r, w):
            mk.op(E, lambda h: h.memset(ap, val), r, w)

        def ASEL(out, in_, pattern, cmp, fill, base, cm, r, w):
            mk.op("pool", lambda h: h.affine_select(out=out, in_=in_, pattern=pattern, compare_op=cmp, fill=fill,
                                                    base=base, channel_multiplier=cm), r, w)

        _evac = [0]

        def EV():
            _evac[0] += 1
            return "act" if _evac[0] % 2 else "dve"

        free_banks = list(range(8))
        rot = [0]

        def bank():
            b = free_banks[rot[0] % len(free_banks)]
            rot[0] += 1
            return b

        def hold(n):
            bs = [free_banks.pop(0) for _ in range(n)]
            return bs

        def release(bs):
            for b in bs:
                free_banks.append(b)
            free_banks.sort()

        def pk(b):
            return ("ps", b)

        def bc(ap, shape):
            return ap.to_broadcast(list(shape))

        def dbg_tap(name, ap, shape, key):
            if name not in dbg:
                return
            d = dout("dbg_" + name, shape)
            dbg_out[name] = d
            mk.dma("sp", d.ap(), ap, reads=[key], key=("dbg", name))

        identf = mk.sb("identf", [128, 128])
        identb = mk.sb("identb", [128, 128], BF16)
        blkones = mk.sb("blkones", [128, 128])
        mLE = {k: mk.sb("mLE" + k, [128, 128]) for k in "PS"}
        mGT = {k: mk.sb("mGT" + k, [128, 128]) for k in "PS"}
        mON = {k: mk.sb("mON" + k, [128, 128]) for k in "PS"}
        mAM = {k: mk.sb("mAM" + k, [128, 512], BF16) for k in "PS"}
        mLEb = {k: mk.sb("mLEb" + k, [128, 128], BF16) for k in "PS"}
        mGTb = {k: mk.sb("mGTb" + k, [128, 128], BF16) for k in "PS"}
        rmask = {"P": mk.sb("rmP", [128, 512]), "S": mk.sb("rmS", [128, 128])}
        seqcol = mk.sb("seqcol", [128, 16, 128], BF16)
        seqrow = mk.sb("seqrow", [128, 16])
        CONSTS = ["identf", "identb", "blkones", "masks", "vecs"]

        def c128(name, nblk):
            return mk.sb("c_" + name, [128, nblk])

        g_mix, g_ffn, g_ple, g_ssdn = c128("g_mix", 8), c128("g_ffn", 8), c128("g_ple", 8), c128("g_ssdn", 8)
        c_cb, c_mu = c128("conv_b", 12), c128("mu", 26)
        c_w0, c_a0, c_kk, c_ka, c_rk, c_lnw, c_lnb = [c128(n, 8) for n in ("w0", "a0", "kk", "ka", "rk", "lnw", "lnb")]
        c_cw = mk.sb("c_cw", [128, 12, 4])
        r_dtb, r_ah, r_dsk = mk.sb("r_dtb", [128, 16]), mk.sb("r_ah", [128, 16]), mk.sb("r_dsk", [128, 16])
        W2A = mk.sb("W2A", [128, 1024], BF16)
        G2 = mk.sb("G2", [128, 1024], BF16)
        WPP = mk.sb("WPP", [128, 2, 1024], BF16)
        NSLOT = 4
        WS = mk.sb("WS", [128, NSLOT, 8, 512], BF16)
        shcarry = mk.sb("shcarry", [128, 26])
        cvcarry = mk.sb("cvcarry", [128, 12, 3])
        hT32 = mk.sb("hT32", [128, 1024])
        hTb = mk.sb("hTb", [128, 1024], BF16)
        S32 = mk.sb("S32", [128, 8, 64])
        Sb = mk.sb("Sb", [128, 8, 64], BF16)
        small = mk.sb("small", [128, 64])
        arena = Arena(mk, "arena", 36000)

        def vload(dst, name, nblk):
            mk.dma("sp", dst[:], bass.AP(vec[name], 0, [[1, 128], [128, nblk]]), writes=["vecs"], key="vecs",
                   allow_slow_non_contiguous=True)

        vload(g_mix, "norm_mix", 8); vload(g_ffn, "norm_ffn", 8); vload(g_ple, "norm_ple", 8); vload(g_ssdn, "ssd_norm", 8)
        vload(c_cb, "conv_b", 12); vload(c_mu, "shift_mu", 26)
        vload(c_w0, "w0", 8); vload(c_a0, "a0", 8); vload(c_kk, "k_k", 8); vload(c_ka, "k_a", 8); vload(c_rk, "r_k", 8)
        vload(c_lnw, "ln_x_w", 8); vload(c_lnb, "ln_x_b", 8)
        for j in range(4):
            mk.dma("sp", c_cw[:, :, j], bass.AP(conv_w, j * 1536, [[1, 128], [128, 12]]), writes=["vecs"], key="vecs",
                   allow_slow_non_contiguous=True)
        for dst, nm in ((r_dtb, "dt_bias"), (r_ah, "a_log"), (r_dsk, "d_skip")):
            mk.dma("sp", dst[:], bass.AP(vec[nm], 0, [[0, 128], [1, 16]]), writes=["vecs"], key="vecs")
        ACT(r_ah[:], r_ah[:], AF.Exp, ["vecs"], ["vecs"])
        TS("dve", r_ah[:], r_ah[:], -1.0, ALU.mult, ["vecs"], ["vecs"])
        mk.dma("pool", W2A[0:64, :], w2.ap(), writes=["W2A"])
        mk.dma("pool", W2A[64:128, :], a2.ap(), writes=["W2A"])
        mk.dma("pool", G2[:], g2.ap(), writes=["G2"])
        mk.dma("pool", WPP[:], w_plp.ap().rearrange("(k p) n -> p k n", p=128), writes=["WPP"])

        M = ["masks"]
        MEMSET("pool", identf[:], 0.0, [], M)
        ASEL(identf[:], identf[:], [[-1, 128]], ALU.not_equal, 1.0, 0, 1, M, M)
        CP("pool", identb[:], identf[:], M, M)
        MEMSET("pool", blkones[:], 1.0, [], M)
        bo3 = blkones[:].rearrange("p (c l) -> p c l", l=64)
        ASEL(bo3, bo3, [[-64, 2], [0, 64]], ALU.is_ge, 0.0, 0, 1, M, M)
        ASEL(bo3, bo3, [[64, 2], [0, 64]], ALU.is_ge, 0.0, 63, -1, M, M)
        for k in "PS":
            MEMSET("pool", mLE[k][:], 1.0, [], M)
            ASEL(mLE[k][:], mLE[k][:], [[1, 128]], ALU.is_ge, 0.0, 0, -1, M, M)
            MEMSET("pool", mGT[k][:], 1.0, [], M)
            ASEL(mGT[k][:], mGT[k][:], [[-1, 128]], ALU.is_ge, 0.0, -1, 1, M, M)
            MEMSET("pool", mON[k][:], 1.0, [], M)
            if k == "S":
                for t_ in (mLE[k], mGT[k], mON[k]):
                    v3 = t_[:].rearrange("p (c l) -> p c l", l=8)
                    ASEL(v3, v3, [[-8, 16], [0, 8]], ALU.is_ge, 0.0, 0, 1, M, M)
                    ASEL(v3, v3, [[8, 16], [0, 8]], ALU.is_ge, 0.0, 7, -1, M, M)
            CP("pool", mLEb[k][:], mLE[k][:], M, M)
            CP("pool", mGTb[k][:], mGT[k][:], M, M)
            for q in (1, 3):
                CP("pool", mAM[k][:, q * 128:(q + 1) * 128], mLE[k][:], M, M)
            for q in (0, 2):
                TT("pool", mAM[k][:, q * 128:(q + 1) * 128], mLE[k][:], identf[:], ALU.subtract, M, M)
        MEMSET("pool", rmask["P"][:], 1.0, [], M)
        MEMSET("pool", rmask["P"][:].rearrange("p (c l) -> p c l", l=128)[:, :, 0:1], 0.0, M, M)
        MEMSET("pool", rmask["S"][:], 1.0, [], M)
        MEMSET("pool", rmask["S"][:].rearrange("p (c l) -> p c l", l=8)[:, :, 0:1], 0.0, M, M)
        MEMSET("pool", seqcol[:], 1.0, [], M)
        sc4 = seqcol[:].rearrange("p s (c l) -> p s c l", l=8)
        ASEL(sc4, sc4, [[-1, 16], [1, 16], [0, 8]], ALU.is_ge, 0.0, 0, 0, M, M)
        ASEL(sc4, sc4, [[1, 16], [-1, 16], [0, 8]], ALU.is_ge, 0.0, 0, 0, M, M)
        MEMSET("pool", seqrow[:], 1.0, [], M)
        ASEL(seqrow[:], seqrow[:], [[-8, 16]], ALU.is_ge, 0.0, 0, 1, M, M)
        ASEL(seqrow[:], seqrow[:], [[8, 16]], ALU.is_ge, 0.0, 7, -1, M, M)
        MEMSET("pool", shcarry[:], 0.0, [], ["shcarry"])
        MEMSET("pool", cvcarry[:], 0.0, [], ["cvcarry"])
        MEMSET("pool", hT32[:], 0.0, [], ["hT32"])
        MEMSET("pool", hTb[:], 0.0, [], ["hTb"])
        MEMSET("pool", S32[:], 0.0, [], ["S32"])
        MEMSET("pool", Sb[:], 0.0, [], ["Sb"])
        mk.barrier()

        slot_i = [0]

        def wslot():
            s = slot_i[0] % NSLOT
            slot_i[0] += 1
            return s

        def wload(s, dram, r0, nk, c0, ncol, col_off=0):
            src = dram.ap()[r0:r0 + nk * 128, c0:c0 + ncol].rearrange("(k p) n -> p k n", p=128)
            mk.dma("pool", WS[:, s, 0:nk, col_off:col_off + ncol], src, writes=[("ws", s)])

        def norm_T(src, src_key, gains, dstT, dst_key, col0, tmpk):
            junk = arena_tmp["junk"]
            un = arena_tmp["un"]
            ss, rs = small[:, 0:1], small[:, 1:2]
            ACT(junk, src, AF.Square, [src_key], ["un", "ss"], accum=ss)
            ACT(rs, ss, AF.Ln, ["ss"], ["rs"], scale=1.0 / D, bias=NORM_EPS)
            ACT(rs, rs, AF.Exp, ["rs"], ["rs"], scale=-0.5)
            TS("dve", un, src, rs, ALU.mult, [src_key, "rs"], ["un"])
            for half in range(2):
                b = bank()
                TR([(PS16[:, b, j * 128:(j + 1) * 128], un[:, (half * 4 + j) * 128:(half * 4 + j + 1) * 128]) for j in range(4)],
                   identb[:], ["un"], [pk(b)])
                TT(EV() if False else "dve", dstT[:, half * 4:half * 4 + 4, col0:col0 + 128],
                   PS16[:, b, 0:512].rearrange("p (a t) -> p a t", t=128),
                   bc(gains[:, half * 4:half * 4 + 4].unsqueeze(2), [128, 4, 128]), ALU.mult, [pk(b)], [dst_key])

        arena_tmp = {}

        SEGS = {"P0": ("P", 0, 4), "P1": ("P", 512, 4), "P2": ("P", 1024, 4), "P3": ("P", 1536, 4), "S": ("S", 2048, 1)}
        last_prompt = [s for s in segs_cfg if s.startswith("P")][-1] if any(s.startswith("P") for s in segs_cfg) else None

        for segname in segs_cfg:
            kind, row0, nsub = SEGS[segname]
            K = kind
            NTK = nsub * 128
            nch, Lc = (1, NTK) if kind == "P" else (16, 8)
            ncq, Lq = (nsub, 128) if kind == "P" else (16, 8)
            is_last_p = (segname == last_prompt)
            arena.reset()
            A = arena.alloc
            arena_tmp["un"] = A([1024], BF16)
            arena_tmp["junk"] = arena_tmp["un"]
            x_tm = A([nsub, 1024])
            unT = A([8, NTK], BF16)
            dt_tm = A([nsub, 16]); dtA = A([nsub, 16])
            LoT = A([NTK], BF16); sgT = A([NTK], BF16)
            YT = A([16, NTK], BF16)
            rawb = [A([nch, 1 + Lc]) for _ in range(2)]
            tsd = [A([NTK]) for _ in range(2)]
            if kind == "S":
                stshT = A([26, 16]); stcvT = A([12, 16, 3])
                shnewS = A([26, 16]); cnewS = A([12, 16, 3])
            mark_wkv = arena.off
            zs = A([nsub, 1024], BF16)
            xs_tm = A([nsub, 1024], BF16)
            B_tm = A([nsub, 2, 128], BF16)
            BT = A([2, NTK], BF16); CT = A([2, NTK], BF16)
            mark_seg = arena.off
            h3 = lambda ap, a=16: ap.rearrange("p (a x) -> p a x", a=a)
            r3 = lambda ap: ap.rearrange("p (a t) -> p a t", t=128)

            def seg3(ap2d):
                return ap2d.rearrange("p (c l) -> p c l", l=Lc)

            for sub in range(nsub):
                mk.dma("sp", x_tm[:, sub, :], xcat.ap()[row0 + sub * 128:row0 + (sub + 1) * 128, :], writes=[("x", sub)])
                norm_T(x_tm[:, sub, :], ("x", sub), g_mix, unT, "unT", sub * 128, None)

            if kind == "S":
                sst = A([3328]); scv = A([1536])
                mk.dma("sp", sst[0:16, :], st_shift.ap(), writes=["sst"])
                mk.dma("sp", scv[0:48, :], st_conv.ap(), writes=["scv"])
                for g0 in range(0, 26, 13):
                    b = bank()
                    TR([(PS[:, b, (i - g0) * 16:(i - g0) * 16 + 16], sst[0:16, i * 128:(i + 1) * 128]) for i in range(g0, g0 + 13)],
                       identf[0:16, 0:16], ["sst"], [pk(b)])
                    CP(EV(), stshT[:, g0:g0 + 13, :], PS[:, b, 0:13 * 16].rearrange("p (a s) -> p a s", s=16), [pk(b)], ["stshT"])
                for g0 in range(0, 12, 6):
                    b = bank()
                    TR([(PS[:, b, (i - g0) * 48:(i - g0) * 48 + 48], scv[0:48, i * 128:(i + 1) * 128]) for i in range(g0, g0 + 6)],
                       identf[0:48, 0:48], ["scv"], [pk(b)])
                    CP(EV(), stcvT[:, g0:g0 + 6, :, :], PS[:, b, 0:6 * 48].rearrange("p (a s j) -> p a s j", s=16, j=3),
                       [pk(b)], ["stcvT"])

            if cfg.get("stop") == 1:
                mk.barrier()
                continue
            s = wslot()
            wload(s, w_in, 0, 8, C_DT, 16)
            for sub in range(nsub):
                b = bank()
                MM([(PS[:, b, 0:16], unT[:, dk, sub * 128:(sub + 1) * 128], WS[:, s, dk, 0:16], dk == 0, dk == 7) for dk in range(8)],
                   ["unT", ("ws", s)], [pk(b)])
                t16 = small[:, 16:32]
                TT("dve", t16, PS[:, b, 0:16], r_dtb[:], ALU.add, [pk(b)], ["t16"])
                ACT(t16, t16, AF.Exp, ["t16"], ["t16"])
                ACT(dt_tm[:, sub, :], t16, AF.Ln, ["t16"], ["dt"], bias=1.0)
                TT("dve", dtA[:, sub, :], dt_tm[:, sub, :], r_ah[:], ALU.mult, ["dt"], ["dtA"])
            for half in range(2):
                s = wslot()
                wload(s, w_in, 0, 8, C_Z + half * 512, 512)
                for sub in range(nsub):
                    b = bank()
                    MM([(PS[:, b, :], unT[:, dk, sub * 128:(sub + 1) * 128], WS[:, s, dk, :], dk == 0, dk == 7) for dk in range(8)],
                       ["unT", ("ws", s)], [pk(b)])
                    ACT(zs[:, sub, half * 512:(half + 1) * 512], PS[:, b, :], AF.Silu, [pk(b)], ["zs"])

            craw = [A([nch, 3 + Lc]) for _ in range(2)]
            cacc = [A([NTK]) for _ in range(2)]
            xsT = [A([NTK], BF16) for _ in range(2)]
            for grp in range(3):
                s = wslot()
                wload(s, w_in, 0, 8, C_XBC + grp * 512, 512)
                for bi in range(4):
                    blk = grp * 4 + bi
                    cr, ca, ck = craw[blk % 2], cacc[blk % 2], ("craw", blk % 2)
                    b = bank()
                    MM([(PS[:, b, 0:NTK], WS[:, s, dk, bi * 128:(bi + 1) * 128], unT[:, dk, :], dk == 0, dk == 7) for dk in range(8)],
                       ["unT", ("ws", s)], [pk(b)])
                    CP("act", cr[:, :, 3:3 + Lc], seg3(PS[:, b, 0:NTK]), [pk(b)], [ck])
                    if kind == "P":
                        CP("dve", cr[:, 0, 0:3], cvcarry[:, blk, :], ["cvcarry"], [ck])
                        CP("dve", cvcarry[:, blk, :], cr[:, 0, Lc:Lc + 3], [ck], ["cvcarry"])
                    else:
                        CP("dve", cr[:, :, 0:3], stcvT[:, blk, :, :], ["stcvT"], [ck])
                        CP("dve", cnewS[:, blk, :, :], cr[:, :, Lc:Lc + 3], [ck], ["cnewS"])
                    ak = ("cacc", blk % 2)
                    ca3 = seg3(ca)
                    TS("dve", ca3, cr[:, :, 0:Lc], c_cw[:, blk, 0:1], ALU.mult, [ck], [ak])
                    for j in range(1, 4):
                        STT("dve", ca3, cr[:, :, j:j + Lc], c_cw[:, blk, j:j + 1], ca3, ALU.mult, ALU.add, [ck, ak], [ak])
                    if blk < 8:
                        xt, xk = xsT[blk % 2], ("xsT", blk % 2)
                        ACT(xt, ca, AF.Silu, [ak], [xk], bias=c_cb[:, blk:blk + 1])
                        b2 = bank()
                        TR([(PS16[:, b2, i * 128:(i + 1) * 128], xt[:, i * 128:(i + 1) * 128]) for i in range(nsub)],
                           identb[:], [xk], [pk(b2)])
                        CP(EV(), xs_tm[:, :, blk * 128:(blk + 1) * 128], r3(PS16[:, b2, 0:nsub * 128]), [pk(b2)], ["xs_tm"])
                    elif blk < 10:
                        g = blk - 8
                        ACT(BT[:, g, :], ca, AF.Silu, [ak], ["BT"], bias=c_cb[:, blk:blk + 1])
                        b2 = bank()
                        TR([(PS16[:, b2, i * 128:(i + 1) * 128], BT[:, g, i * 128:(i + 1) * 128]) for i in range(nsub)],
                           identb[:], ["BT"], [pk(b2)])
                        CP(EV(), B_tm[:, :, g, :], r3(PS16[:, b2, 0:nsub * 128]), [pk(b2)], ["B_tm"])
                    else:
                        g = blk - 10
                        ACT(CT[:, g, :], ca, AF.Silu, [ak], ["CT"], bias=c_cb[:, blk:blk + 1])

            tsi = [0]

            def tshift(b, rwblk, out_u, out_key):
                i = tsi[0] % 2
                tsi[0] += 1
                rb, td, rk_, tk_ = rawb[i], tsd[i], ("rawb", i), ("tsd", i)
                CP("act", rb[:, :, 1:1 + Lc], seg3(PS[:, b, 0:NTK]), [pk(b)], [rk_])
                if kind == "P":
                    CP("dve", rb[:, 0, 0:1], shcarry[:, rwblk:rwblk + 1], ["shcarry"], [rk_])
                    CP("dve", shcarry[:, rwblk:rwblk + 1], rb[:, 0, Lc:Lc + 1], [rk_], ["shcarry"])
                else:
                    CP("dve", rb[:, :, 0:1], stshT[:, rwblk, :].unsqueeze(2), ["stshT"], [rk_])
                    CP("dve", shnewS[:, rwblk, :].unsqueeze(2), rb[:, :, Lc:Lc + 1], [rk_], ["shnewS"])
                TT("dve", seg3(td), rb[:, :, 0:Lc], rb[:, :, 1:1 + Lc], ALU.subtract, [rk_], [tk_])
                STT("dve", seg3(out_u), seg3(td), c_mu[:, rwblk:rwblk + 1], rb[:, :, 1:1 + Lc], ALU.mult, ALU.add,
                    [tk_, rk_], [out_key])

            s = wslot()
            wload(s, w_in, 0, 8, C_RW + 3072, 256)
            ulo = A([NTK])
            for bi in range(2):
                b = bank()
                MM([(PS[:, b, 0:NTK], WS[:, s, dk, bi * 128:(bi + 1) * 128], unT[:, dk, :], dk == 0, dk == 7) for dk in range(8)],
                   ["unT", ("ws", s)], [pk(b)])
                tshift(b, 24 + bi, ulo, "ulo")
                if bi == 0:
                    ACT(LoT[0:64, :], ulo[0:64, :], AF.Tanh, ["ulo"], ["LoT"])
                    CP("dve", LoT[64:128, :], ulo[64:128, :], ["ulo"], ["LoT"])
                else:
                    ACT(sgT, ulo, AF.Sigmoid, ["ulo"], ["sgT"])

            if cfg.get("stop") == 2:
                mk.barrier()
                continue
            mk.barrier()
            arena.off = mark_seg
            ea, coef, cdd = A([16]), A([16]), A([16])
            CBm = A([2, 128], BF16)
            L1s = [A([4, 128]) for _ in range(2)]; Ebs = [A([4, 128]) for _ in range(2)]; WTs = [A([4, 128], BF16) for _ in range(2)]
            yv = A([1024]); t2 = A([1024]); yn = A([1024], BF16); xpp = A([1024], BF16)
            junk32 = A([1024])
            ss2, rs2 = small[:, 2:4], small[:, 4:6]
            for sub in range(nsub):
                tsl = slice(sub * 128, (sub + 1) * 128)
                b = bank()
                MM([(PS[:, b, 0:16], mLE[K][:], dtA[:, sub, :], True, True),
                    (PS[:, b, 16:32], mGT[K][:], dtA[:, sub, :], True, True),
                    (PS[:, b, 32:48], mON[K][:], dtA[:, sub, :], True, True)], ["dtA"], [pk(b)])
                ACT(ea, PS[:, b, 0:16], AF.Exp, [pk(b)], ["ea"])
                ACT(coef, PS[:, b, 16:32], AF.Exp, [pk(b)], ["coef"])
                ACT(cdd, PS[:, b, 32:48], AF.Exp, [pk(b)], ["cdd"])
                TT("dve", coef, coef, dt_tm[:, sub, :], ALU.mult, ["coef", "dt"], ["coef"])
                b = bank()
                MM([(PS[:, b, g * 128:(g + 1) * 128], BT[:, g, tsl], CT[:, g, tsl], True, True) for g in range(2)],
                   ["BT", "CT"], [pk(b)])
                TT("dve", CBm, r3(PS[:, b, 0:256]), bc(mLEb[K][:].unsqueeze(1), [128, 2, 128]), ALU.mult, [pk(b)], ["CBm"])
                yb = hold(2)
                for hg in range(4):
                    L1, Eb, WT = L1s[hg % 2], Ebs[hg % 2], WTs[hg % 2]
                    kL1, kEb, kWT = ("L1", hg % 2), ("Eb", hg % 2), ("WT", hg % 2)
                    for i in range(4):
                        h_ = hg * 4 + i
                        TS("dve", L1[:, i, :], mGT[K][:], dtA[:, sub, h_:h_ + 1], ALU.mult, ["dtA"], [kL1])
                    b = bank()
                    MM([(PS[:, b, i * 128:(i + 1) * 128], L1[:, i, :], mLE[K][:], True, True) for i in range(4)], [kL1], [pk(b)])
                    ACT(Eb.rearrange("p a t -> p (a t)"), PS[:, b, :], AF.Exp, [pk(b)], [kEb])
                    TT("dve", Eb, Eb, bc(dt_tm[:, sub, hg * 4:hg * 4 + 4].unsqueeze(2), [128, 4, 128]), ALU.mult, [kEb, "dt"], [kEb])
                    g = hg // 2
                    TT("dve", WT, Eb, bc(CBm[:, g, :].unsqueeze(1), [128, 4, 128]), ALU.mult, [kEb, "CBm"], [kWT])
                    MM([(PS[:, yb[g], ((hg * 4 + i) % 8) * 64:((hg * 4 + i) % 8) * 64 + 64], WT[:, i, :],
                         xs_tm[:, sub, (hg * 4 + i) * 64:(hg * 4 + i + 1) * 64], True, True) for i in range(4)],
                       [kWT, "xs_tm"], [("yb", g, hg % 2)])
                TT("dve", h3(xpp), h3(xs_tm[:, sub, :]), bc(coef.unsqueeze(2), [128, 16, 64]), ALU.mult, ["xs_tm", "coef"], ["xpp"])
                ob = hold(2)
                if K == "P":
                    for g in range(2):
                        MM([(PS[:, ob[g], :], CT[:, g, tsl], hTb[:, g * 512:(g + 1) * 512], True, True)], ["CT", "hTb"], [pk(ob[g])])
                else:
                    rhsj = A([2, 16, 8]); cdP = A([16, 8])
                    S0n = [A([8, 128]) for _ in range(2)]
                    h0T = A([8, 128], BF16); CTm = A([2, 128], BF16); xppm = A([1024], BF16)
                    Snew = [A([8, 128]) for _ in range(2)]
                    dq = dtA[:, 0, :].rearrange("p (q j) -> p q j", j=2)
                    for j in range(2):
                        TT("dve", rhsj[:, j, :, :], bc(dq[:, :, j].unsqueeze(1), [128, 16, 8]), bc(seqrow[:, :].unsqueeze(2), [128, 16, 8]),
                           ALU.mult, ["dtA"], ["rhsj"])
                    b = bank()
                    for j in range(2):
                        MM([(PS[j * 64:(j + 1) * 64, b, 0:128], mON["P"][:, 0:64], rhsj[:, j, :, :].rearrange("p s q -> p (s q)"), True, True)],
                           ["rhsj"], [pk(b)])
                    ACT(cdP.rearrange("p s q -> p (s q)"), PS[:, b, 0:128], AF.Exp, [pk(b)], ["cdP"])
                    for seq in range(16):
                        sn, snk = S0n[seq % 2], ("S0n", seq % 2)
                        src4 = st_ssm.ap()[seq].rearrange("(q j) p n -> j p q n", j=2)
                        for j in range(2):
                            mk.dma("sp", sn[j * 64:(j + 1) * 64, :, :], src4[j], writes=[snk])
                        for half in range(2):
                            b = bank()
                            TR([(PS[:, b, i * 128:(i + 1) * 128], sn[:, half * 4 + i, :]) for i in range(4)], identf[:], [snk], [pk(b)])
                            CP(EV(), h0T[:, half * 4:half * 4 + 4, :], r3(PS[:, b, :]), [pk(b)], ["h0T"])
                        TT("dve", CTm, CT.rearrange("p g t -> p g t"), bc(seqcol[:, seq, :].unsqueeze(1), [128, 2, 128]), ALU.mult,
                           ["CT"], ["CTm"])
                        h0f = h0T.rearrange("p q x -> p (q x)")
                        for g in range(2):
                            MM([(PS[:, ob[g], :], CTm[:, g, :], h0f[:, g * 512:(g + 1) * 512], seq == 0, seq == 15)],
                               ["CTm", "h0T"], [pk(ob[g])])
                        TS("dve", xppm, xpp, seqrow[:, seq:seq + 1], ALU.mult, ["xpp"], ["xppm"])
                        dh = [bank(), bank()]
                        for q in range(8):
                            MM([(PS[:, dh[q // 4], (q % 4) * 128:(q % 4 + 1) * 128], xppm[:, q * 128:(q + 1) * 128],
                                 B_tm[:, 0, q // 4, :], True, True)], ["xppm", "B_tm"], [("dh", dh[q // 4]), pk(dh[q // 4])])
                        so, sok = Snew[seq % 2], ("Snew", seq % 2)
                        TT("dve", so, sn, bc(cdP[:, seq, :].unsqueeze(2), [128, 8, 128]), ALU.mult, [snk, "cdP"], [sok])
                        for half in range(2):
                            TT("dve", so[:, half * 4:half * 4 + 4, :], so[:, half * 4:half * 4 + 4, :], r3(PS[:, dh[half], :]), ALU.add,
                               [sok, ("dh", dh[half])], [sok, pk(dh[half])])
                        dst4 = o_ssm_s.ap()[seq].rearrange("(q j) p n -> j p q n", j=2)
                        for j in range(2):
                            mk.dma("pool", dst4[j], so[j * 64:(j + 1) * 64, :, :], reads=[sok], key=sok)
                for g in range(2):
                    yg = yv[:, g * 512:(g + 1) * 512]
                    TT("dve", h3(yg, 8), h3(PS[:, ob[g], :], 8), bc(ea[:, g * 8:g * 8 + 8].unsqueeze(2), [128, 8, 64]), ALU.mult,
                       [pk(ob[g]), "ea"], ["yv"])
                    TT("dve", yg, yg, PS[:, yb[g], :], ALU.add, ["yv", ("yb", g, 0), ("yb", g, 1)], ["yv"])
                release(ob); release(yb)
                TT("dve", h3(t2), h3(xs_tm[:, sub, :]), bc(r_dsk[:].unsqueeze(2), [128, 16, 64]), ALU.mult, ["xs_tm"], ["t2"])
                TT("dve", yv, yv, t2, ALU.add, ["yv", "t2"], ["yv"])
                TT("dve", yv, yv, zs[:, sub, :], ALU.mult, ["yv", "zs"], ["yv"])
                for g in range(2):
                    ACT(junk32[:, 0:512], yv[:, g * 512:(g + 1) * 512], AF.Square, ["yv"], ["junk32", "ss2"], accum=ss2[:, g:g + 1])
                ACT(rs2, ss2, AF.Ln, ["ss2"], ["rs2"], scale=1.0 / 512, bias=NORM_EPS)
                ACT(rs2, rs2, AF.Exp, ["rs2"], ["rs2"], scale=-0.5)
                for g in range(2):
                    TS("dve", yn[:, g * 512:(g + 1) * 512], yv[:, g * 512:(g + 1) * 512], rs2[:, g:g + 1], ALU.mult, ["yv", "rs2"], ["yn"])
                for half in range(2):
                    b = bank()
                    TR([(PS16[:, b, j * 128:(j + 1) * 128], yn[:, (half * 4 + j) * 128:(half * 4 + j + 1) * 128]) for j in range(4)],
                       identb[:], ["yn"], [pk(b)])
                    TT("dve", YT[:, half * 4:half * 4 + 4, tsl], r3(PS16[:, b, 0:512]),
                       bc(g_ssdn[:, half * 4:half * 4 + 4].unsqueeze(2), [128, 4, 128]), ALU.mult, [pk(b)], ["YT"])
                if K == "P":
                    for g in range(2):
                        b = bank()
                        MM([(PS[:, b, :], B_tm[:, sub, g, :], xpp[:, g * 512:(g + 1) * 512], True, True)], ["B_tm", "xpp"], [pk(b)])
                        hg_ = hT32[:, g * 512:(g + 1) * 512]
                        TT("dve", h3(hg_, 8), h3(hg_, 8), bc(cdd[:, g * 8:g * 8 + 8].unsqueeze(2), [128, 8, 64]), ALU.mult,
                           ["hT32", "cdd"], ["hT32"])
                        TT("dve", hg_, hg_, PS[:, b, :], ALU.add, ["hT32", pk(b)], ["hT32"])
                    CP("act", hTb[:], hT32[:], ["hT32"], ["hTb"])

            if cfg.get("stop") == 3:
                mk.barrier()
                continue
            mk.barrier()
            arena.off = mark_wkv
            NCB = 4 if K == "P" else 2
            NH = 2 * NCB
            NG = 8 // NCB
            ARt = A([NCB, nsub, 2, 128], BF16)
            KtT = A([NCB, NTK], BF16); BtT = A([NCB, NTK], BF16)
            V_tm = A([nsub, NH * 64], BF16); Kh_tm = A([nsub, NH * 64], BF16); Bh_tm = A([nsub, NH * 64], BF16)
            bvT = A([NCB, NTK], BF16)
            gamL = A([8, ncq])
            f32n = lambda: A([NTK])
            u_r, u_k, u_v = f32n(), f32n(), f32n()
            sw, asg, kk0, sq, kk, kmod, bvec, Gs, D1, GH = [f32n() for _ in range(10)]
            KhT, BhT, vb = A([NTK], BF16), A([NTK], BF16), A([NTK], BF16)
            off_AM = arena.off
            AM = A([NH, 512], BF16)
            Pb = [A([NH, 128], BF16) for _ in range(2)]
            Qb = [A([NH, 128], BF16) for _ in range(2)]
            Zb = A([NH, 128], BF16)
            Xs = A([NH * 64], BF16); Us = A([NH * 64], BF16)
            o_g = A([NH * 64]); onb = A([NH * 64], BF16); sqg = A([NH * 64])
            onf = sqg
            t1 = sqg.rearrange("p (c x) -> p c x", x=128)
            Stmp = A([NCB, 64])
            gs1, gs2, gmean, gm2, gvar, grstd = [A([NH]) for _ in range(6)]
            NB2 = NH * 128 // 512

            def bank2():
                if NB2 == 1:
                    return bank()
                bb = bank()
                while bb % 2:
                    bb = bank()
                rot[0] += 1
                return bb

            def pk2(b_):
                return [pk(b_)] if NB2 == 1 else [pk(b_), pk(b_ + 1)]

            def PSn(b_):
                return PS[:, b_:b_ + NB2, :].rearrange("p b (c x) -> p (b c) x", x=128)
            rm = rmask[kind]
            nit = 6 if K == "P" else 2
            if K == "S":
                SbS = A([16, 8, 64], BF16); S32S = A([16, 8, 64])
                ARm = A([2, 16, 2, 128], BF16)
                U_all = A([1024], BF16); V_all = A([1024], BF16); Kh_all = A([1024], BF16); Bh_all = A([1024], BF16)
                S0w = [A([8, 128]) for _ in range(2)]
                for seq in range(16):
                    sw_, swk = S0w[seq % 2], ("S0w", seq % 2)
                    srcw = st_wkv.ap()[seq].rearrange("(c j) v k -> j v c k", j=2)
                    for j in range(2):
                        mk.dma("sp", sw_[0:64, :, j * 64:(j + 1) * 64], srcw[j], writes=[swk])
                    b = bank()
                    TR([(PS[:, b, i * 64:(i + 1) * 64], sw_[0:64, i, :]) for i in range(8)], identf[0:64, 0:64], [swk], [pk(b)])
                    CP("dve", S32S[:, seq, :, :], PS[:, b, :].rearrange("p (c v) -> p c v", v=64), [pk(b)], ["S32S"])
                    CP("act", SbS[:, seq, :, :], S32S[:, seq, :, :], ["S32S"], ["SbS"])

            class _Stop(Exception):
                pass
            try:
              if cfg.get("stop") == 30:
                  raise _Stop()
              for hg in range(NG):
                  for cbl in range(NCB):
                      cb = NCB * hg + cbl
                      cs = slice(cb, cb + 1)
                      s = wslot()
                      for q in range(3):
                          wload(s, w_in, 0, 8, C_RW + q * 1024 + cb * 128, 128, col_off=q * 128)
                      for q, (uq, uk_) in enumerate(((u_r, "u_r"), (u_k, "u_k"), (u_v, "u_v"))):
                          b = bank()
                          MM([(PS[:, b, 0:NTK], WS[:, s, dk, q * 128:(q + 1) * 128], unT[:, dk, :], dk == 0, dk == 7) for dk in range(8)],
                             ["unT", ("ws", s)], [pk(b)])
                          tshift(b, q * 8 + cb, uq, uk_)
                      b = bank()
                      MM([(PS[:, b, 0:NTK], W2A[0:64, cb * 128:(cb + 1) * 128], LoT[0:64, :], True, True)], ["W2A", "LoT"], [pk(b)])
                      ACT(sw, PS[:, b, 0:NTK], AF.Sigmoid, [pk(b)], ["sw"], bias=c_w0[:, cs])
                      b = bank()
                      MM([(PS[:, b, 0:NTK], W2A[64:128, cb * 128:(cb + 1) * 128], LoT[64:128, :], True, True)], ["W2A", "LoT"], [pk(b)])
                      ACT(asg, PS[:, b, 0:NTK], AF.Sigmoid, [pk(b)], ["asg"], bias=c_a0[:, cs])
                      TS("dve", kk0, u_k, c_kk[:, cs], ALU.mult, ["u_k"], ["kk0"])
                      TT("dve", sq, kk0, kk0, ALU.mult, ["kk0"], ["sq"])
                      b = bank()
                      MM([(PS[:, b, 0:NTK], blkones[:], sq, True, True)], ["sq"], [pk(b)])
                      TS("dve", sq, PS[:, b, 0:NTK], 1e-24, ALU.max, [pk(b)], ["sq"])
                      ACT(sq, sq, AF.Ln, ["sq"], ["sq"])
                      ACT(sq, sq, AF.Exp, ["sq"], ["sq"], scale=-0.5)
                      TT("dve", kk, kk0, sq, ALU.mult, ["kk0", "sq"], ["kk"])
                      TS("dve", kmod, asg, c_ka[:, cs], ALU.mult, ["asg"], ["kmod"], s2=c_ka[:, cs], op1=ALU.subtract)
                      STT("dve", kmod, kmod, 1.0, u_k, ALU.add, ALU.mult, ["kmod", "u_k"], ["kmod"])
                      TT("dve", bvec, kk, asg, ALU.mult, ["kk", "asg"], ["bvec"])
                      SCAN(Gs, rm[:, 0:NTK], sw, ["sw"], ["Gs"])
                      TT("dve", D1, Gs, sw, ALU.subtract, ["Gs", "sw"], ["D1"])
                      G3 = Gs.rearrange("p (c l) -> p c l", l=Lq)
                      GH3 = GH.rearrange("p (c l) -> p c l", l=Lq)
                      TT("dve", GH3, bc(G3[:, :, Lq - 1:Lq], [128, ncq, Lq]), G3, ALU.subtract, ["Gs"], ["GH"])
                      eNG, eD1, eGH, eG = kk0, D1, GH, Gs
                      ACT(eNG, Gs, AF.Exp, ["Gs", "kk"], ["kk0"], scale=CEXP)
                      ACT(eD1, D1, AF.Exp, ["D1"], ["D1"], scale=-CEXP)
                      ACT(eGH, GH, AF.Exp, ["GH"], ["GH"], scale=-CEXP)
                      ACT(eG, Gs, AF.Exp, ["Gs", "kk0", "D1", "GH"], ["Gs"], scale=-CEXP)
                      CP("dve", gamL[:, cb, :].unsqueeze(2), G3[:, :, Lq - 1:Lq], ["Gs"], ["gamL"])
                      TT("dve", ARt[:, cbl, :, 1, :], r3(u_r), r3(eG), ALU.mult, ["u_r", "Gs"], ["ARt"])
                      STT("dve", ARt[:, cbl, :, 0, :], r3(kk), -1.0, r3(eD1), ALU.mult, ALU.mult, ["kk", "D1"], ["ARt"])
                      TT("dve", KtT[:, cbl, :], kmod, eNG, ALU.mult, ["kmod", "kk0"], ["KtT"])
                      TT("dve", BtT[:, cbl, :], bvec, eNG, ALU.mult, ["bvec", "kk0"], ["BtT"])
                      TT("dve", KhT, kmod, eGH, ALU.mult, ["kmod", "GH"], ["KhT"])
                      TT("dve", BhT, bvec, eGH, ALU.mult, ["bvec", "GH"], ["BhT"])
                      CP("act", vb, u_v, ["u_v"], ["vb"])
                      for (srcT, sk_, dst, dk_) in ((KhT, "KhT", Kh_tm, "Kh_tm"), (BhT, "BhT", Bh_tm, "Bh_tm"), (vb, "vb", V_tm, "V_tm")):
                          b2 = bank()
                          TR([(PS16[:, b2, i * 128:(i + 1) * 128], srcT[:, i * 128:(i + 1) * 128]) for i in range(nsub)],
                             identb[:], [sk_], [pk(b2)])
                          CP(EV(), dst[:, :, cbl * 128:(cbl + 1) * 128], r3(PS16[:, b2, 0:nsub * 128]), [pk(b2)], [dk_])
                      prod = sq
                      STT("dve", prod, u_r, c_rk[:, cs], kmod, ALU.mult, ALU.mult, ["u_r", "kmod", "sq"], ["sq"])
                      b = bank()
                      MM([(PS[:, b, 0:NTK], blkones[:], prod, True, True)], ["sq"], [pk(b)])
                      TT("dve", bvT[:, cbl, :], PS[:, b, 0:NTK], u_v, ALU.mult, [pk(b), "u_v"], ["bvT"])

                  if cfg.get("stop") == 305:
                      raise _Stop()
                  if K == "S":
                      for cbl in range(NCB):
                          TT("dve", ARm[:, cbl, :, :, :], bc(ARt[:, cbl, 0, :, :].unsqueeze(1), [128, 16, 2, 128]),
                             bc(seqcol[:, :, :].unsqueeze(2), [128, 16, 2, 128]), ALU.mult, ["ARt"], ["ARm"])
                  if cfg.get("stop") == 31:
                      raise _Stop()
                  heads = [(cbl, j) for cbl in range(NCB) for j in (0, 1)]
                  for sub in range(nsub):
                      tsl = slice(sub * 128, (sub + 1) * 128)
                      bN = bank2()
                      nitems = []
                      for i, (cbl, j) in enumerate(heads):
                          hs = slice(j * 64, (j + 1) * 64)
                          bA = bank()
                          ar = ARt[hs, cbl, sub, :, :].rearrange("p a t -> p (a t)")
                          MM([(PS[:, bA, 0:256], BtT[hs, cbl, tsl], ar, True, True),
                              (PS[:, bA, 256:512], KtT[hs, cbl, tsl], ar, True, True)], ["BtT", "KtT", "ARt"], [pk(bA)])
                          TT("dve", AM[:, i, :], PS[:, bA, :], mAM[K][:], ALU.mult, [pk(bA)], ["AM"])
                          nitems.append((PSn(bN)[:, i, :], ARt[hs, cbl, sub, 0, :], BtT[hs, cbl, tsl], True, True))
                      if cfg.get("stop") == 321:
                          raise _Stop()
                      MM(nitems, ["ARt", "BtT"], pk2(bN))
                      TT("dve", Pb[0], PSn(bN), bc(mGTb[K][:].unsqueeze(1), [128, NH, 128]), ALU.mult, pk2(bN), [("Pb", 0)])
                      CP("act", Zb, bc(identb[:].unsqueeze(1), [128, NH, 128]), [], ["Zb"])
                      if cfg.get("stop") == 322:
                          raise _Stop()
                      Pc, Pk = Pb[0], ("Pb", 0)
                      Qc, Qk = AM[:, :, 0:128], "AM"
                      for it in range(nit + 1):
                          bZ = bank2()
                          MM([(PSn(bZ)[:, i, :], Pc[:, i, :], Zb[:, i, :], True, True) for i in range(NH)], [Pk, "Zb"], pk2(bZ))
                          if it < nit:
                              bP = bank2()
                              MM([(PSn(bP)[:, i, :], Qc[:, i, :], Pc[:, i, :], True, True) for i in range(NH)], [Pk, Qk], pk2(bP))
                              Pn, Pnk = Pb[(it + 1) % 2], ("Pb", (it + 1) % 2)
                          if it < nit - 1:
                              bQ = bank2()
                              MM([(PSn(bQ)[:, i, :], Pc[:, i, :], Qc[:, i, :], True, True) for i in range(NH)], [Pk, Qk], pk2(bQ))
                              Qn, Qnk = Qb[it % 2], ("Qb", it % 2)
                          TT("dve", Zb, Zb, PSn(bZ), ALU.add, ["Zb"] + pk2(bZ), ["Zb"])
                          if it < nit:
                              CP("act", Pn, PSn(bP), pk2(bP), [Pnk])
                          if it < nit - 1:
                              if NB2 == 2:
                                  hh = NH // 2
                                  CP("act", Qn[:, 0:hh, :], r3(PS[:, bQ, :]), [pk(bQ)], [Qnk])
                                  CP("dve", Qn[:, hh:NH, :], r3(PS[:, bQ + 1, :]), [pk(bQ + 1)], [Qnk])
                              else:
                                  CP("dve", Qn, PSn(bQ), pk2(bQ), [Qnk])
                          if it < nit:
                              Pc, Pk = Pn, Pnk
                          if it < nit - 1:
                              Qc, Qk = Qn, Qnk
                      if cfg.get("stop") == 32:
                          raise _Stop()
                      bX = bank()
                      items = []
                      for i, (cbl, j) in enumerate(heads):
                          cb = NCB * hg + cbl
                          hs = slice(j * 64, (j + 1) * 64)
                          hc = slice(i * 64, (i + 1) * 64)
                          o_ = PS[:, bX, i * 64:(i + 1) * 64]
                          if K == "P":
                              items.append((o_, ARt[hs, cbl, sub, 0, :], Sb[hs, cb, :], True, False))
                          else:
                              for sq_ in range(16):
                                  items.append((o_, ARm[hs, cbl, sq_, 0, :], SbS[hs, sq_, cb, :], sq_ == 0, False))
                          items.append((o_, AM[:, i, 256:384], V_tm[:, sub, hc], False, True))
                      MM(items, ["ARt", "Sb", "AM", "V_tm", "ARm", "SbS"], [pk(bX)])
                      CP("act", Xs, PS[:, bX, 0:NH * 64], [pk(bX)], ["Xs"])
                      bU = bank()
                      MM([(PS[:, bU, i * 64:(i + 1) * 64], Zb[:, i, :], Xs[:, i * 64:(i + 1) * 64], True, True) for i in range(NH)],
                         ["Zb", "Xs"], [pk(bU)])
                      CP("act", Us, PS[:, bU, 0:NH * 64], [pk(bU)], ["Us"])
                      bO = bank()
                      items = []
                      for i, (cbl, j) in enumerate(heads):
                          cb = NCB * hg + cbl
                          hs = slice(j * 64, (j + 1) * 64)
                          hc = slice(i * 64, (i + 1) * 64)
                          o_ = PS[:, bO, i * 64:(i + 1) * 64]
                          if K == "P":
                              items.append((o_, ARt[hs, cbl, sub, 1, :], Sb[hs, cb, :], True, False))
                          else:
                              for sq_ in range(16):
                                  items.append((o_, ARm[hs, cbl, sq_, 1, :], SbS[hs, sq_, cb, :], sq_ == 0, False))
                          items.append((o_, AM[:, i, 128:256], Us[:, hc], False, False))
                          items.append((o_, AM[:, i, 384:512], V_tm[:, sub, hc], False, True))
                      MM(items, ["ARt", "Sb", "AM", "V_tm", "Us", "ARm", "SbS"], [pk(bO)])
                      CP("dve", o_g, PS[:, bO, 0:NH * 64], [pk(bO)], ["o_g"])
                      if cfg.get("stop") == 33:
                          raise _Stop()
                      if K == "P":
                          bS = bank()
                          sitems = []
                          for cbl in range(NCB):
                              dst = PS[:, bS, cbl * 128:(cbl + 1) * 128]
                              cc = slice(cbl * 128, (cbl + 1) * 128)
                              sitems += [(dst, Bh_tm[:, sub, cc], Us[:, cc], True, False), (dst, Kh_tm[:, sub, cc], V_tm[:, sub, cc], False, True)]
                          MM(sitems, ["Bh_tm", "Kh_tm", "Us", "V_tm"], [pk(bS)])
                          Sg = S32[:, NCB * hg:NCB * hg + NCB, :]
                          TT("dve", Stmp, Sg, bc(gamL[:, NCB * hg:NCB * hg + NCB, sub:sub + 1], [128, NCB, 64]), ALU.mult, ["S32", "gamL"], ["Stmp"])
                          dsv = PS[:, bS, 0:NCB * 128].rearrange("p (c x) -> p c x", x=128)
                          for j in range(2):
                              hs = slice(j * 64, (j + 1) * 64)
                              TT("dve", S32[hs, NCB * hg:NCB * hg + NCB, :], Stmp[hs, :, :], dsv[hs, :, j * 64:(j + 1) * 64], ALU.add,
                                 ["Stmp", pk(bS)], ["S32"])
                          CP("act", Sb[:, NCB * hg:NCB * hg + NCB, :], Sg, ["S32"], ["Sb"])
                      else:
                          gc = slice(hg * 256, (hg + 1) * 256)
                          CP("act", U_all[:, gc], Us, ["Us"], ["U_all"])
                          CP("act", V_all[:, gc], V_tm[:, 0, :], ["V_tm"], ["V_all"])
                          CP("dve", Kh_all[:, gc], Kh_tm[:, 0, :], ["Kh_tm"], ["Kh_all"])
                          CP("dve", Bh_all[:, gc], Bh_tm[:, 0, :], ["Bh_tm"], ["Bh_all"])
                      if cfg.get("stop") == 34:
                          raise _Stop()
                      RED("dve", gs1, h3(o_g, NH), ["o_g"], ["gs1"])
                      ACT(sqg, o_g, AF.Square, ["o_g"], ["sqg"])
                      RED("dve", gs2, h3(sqg, NH), ["sqg"], ["gs2"])
                      TS("dve", gmean, gs1, 1.0 / 64, ALU.mult, ["gs1"], ["gmean"])
                      TT("dve", gm2, gmean, gmean, ALU.mult, ["gmean"], ["gm2"])
                      STT("dve", gvar, gs2, 1.0 / 64, gm2, ALU.mult, ALU.subtract, ["gs2", "gm2"], ["gvar"])
                      ACT(grstd, gvar, AF.Ln, ["gvar"], ["grstd"], bias=GN_EPS)
                      ACT(grstd, grstd, AF.Exp, ["grstd"], ["grstd"], scale=-0.5)
                      TT("dve", h3(onf, NH), h3(o_g, NH), bc(gmean.unsqueeze(2), [128, NH, 64]), ALU.subtract, ["o_g", "gmean"], ["sqg"])
                      TT("dve", h3(onb, NH), h3(onf, NH), bc(grstd.unsqueeze(2), [128, NH, 64]), ALU.mult, ["sqg", "grstd"], ["onb"])
                      cbs = slice(NCB * hg, NCB * hg + NCB)
                      bT = bank()
                      TR([(PS16[:, bT, i * 128:(i + 1) * 128], onb[:, i * 128:(i + 1) * 128]) for i in range(NCB)], identb[:], ["onb"], [pk(bT)])
                      bG = bank()
                      MM([(PS[:, bG, i * 128:(i + 1) * 128], G2[:, (NCB * hg + i) * 128:(NCB * hg + i + 1) * 128], sgT[:, tsl], True, True)
                          for i in range(NCB)], ["G2", "sgT"], [pk(bG)])
                      TT("dve", t1, r3(PS16[:, bT, 0:NCB * 128]), bc(c_lnw[:, cbs].unsqueeze(2), [128, NCB, 128]), ALU.mult, [pk(bT)], ["sqg"])
                      TT("dve", t1, t1, bc(c_lnb[:, cbs].unsqueeze(2), [128, NCB, 128]), ALU.add, ["sqg"], ["sqg"])
                      TT("dve", t1, t1, bvT[:, :, tsl], ALU.add, ["sqg", "bvT"], ["sqg"])
                      TT("dve", YT[:, 8 + NCB * hg:8 + NCB * hg + NCB, tsl], t1, r3(PS[:, bG, 0:NCB * 128]), ALU.mult, ["sqg", pk(bG)], ["YT"])
            except _Stop:
                stopped = True
            else:
                stopped = False
            if K == "S" and not (stopped and cfg.get("stop", 0) < 40):
                mk.barrier()
                off_keep = arena.off
                arena.off = off_AM
                Bhm = A([1024], BF16); Khm = A([1024], BF16)
                Stm = A([8, 64]); Sn2 = A([8, 64])
                Sout = [A([8, 128]) for _ in range(2)]
                for seq in range(16):
                    TS("dve", Bhm, Bh_all, seqrow[:, seq:seq + 1], ALU.mult, ["Bh_all"], ["Bhm"])
                    TS("dve", Khm, Kh_all, seqrow[:, seq:seq + 1], ALU.mult, ["Kh_all"], ["Khm"])
                    ds = hold(2)
                    for cb in range(8):
                        dst = PS[:, ds[cb // 4], (cb % 4) * 128:(cb % 4 + 1) * 128]
                        cc = slice(cb * 128, (cb + 1) * 128)
                        MM([(dst, Bhm[:, cc], U_all[:, cc], True, False), (dst, Khm[:, cc], V_all[:, cc], False, True)],
                           ["Bhm", "Khm", "U_all", "V_all"], [pk(ds[cb // 4])])
                    TT("dve", Stm, S32S[:, seq, :, :], bc(gamL[:, :, seq:seq + 1], [128, 8, 64]), ALU.mult, ["S32S", "gamL"], ["Stm"])
                    dsv = PS[:, ds[0]:ds[0] + 2, :].rearrange("p b (c x) -> p (b c) x", x=128)
                    for j in range(2):
                        hs = slice(j * 64, (j + 1) * 64)
                        TT("dve", Sn2[hs, :, :], Stm[hs, :, :], dsv[hs, :, j * 64:(j + 1) * 64], ALU.add,
                           ["Stm", pk(ds[0]), pk(ds[1])], ["Sn2"])
                    release(ds)
                    so, sok = Sout[seq % 2], ("Sout", seq % 2)
                    for half in range(2):
                        b = bank()
                        TR([(PS[0:64, b, i * 128:(i + 1) * 128], Sn2[:, half * 4 + i, :]) for i in range(4)], identf[:], ["Sn2"], [pk(b)])
                        CP(EV(), so[0:64, half * 4:half * 4 + 4, :], r3(PS[0:64, b, :]), [pk(b)], [sok])
                    mk.dma("pool", o_wkv_s.ap()[seq].rearrange("(c j) v k -> v c j k", j=2),
                           so[0:64, :, :].rearrange("p c (j k) -> p c j k", j=2), reads=[sok], key=sok)
                mk.barrier()
                arena.off = mark_seg
                stc = A([1536]); sth = A([3328]); cn2 = A([48])
                for blk in range(12):
                    CP("dve", cn2.rearrange("p (s j) -> p s j", j=3), cnewS[:, blk, :, :], ["cnewS"], ["cn2"])
                    b = bank()
                    TR([(PS[0:48, b, 0:128], cn2)], identf[:], ["cn2"], [pk(b)])
                    CP(EV(), stc[0:48, blk * 128:(blk + 1) * 128], PS[0:48, b, 0:128], [pk(b)], ["stc"])
                mk.dma("sp", o_conv_s.ap(), stc[0:48, :], reads=["stc"], key="o_conv_s")
                for g0 in range(0, 26, 4):
                    n_ = min(4, 26 - g0)
                    b = bank()
                    TR([(PS[0:16, b, i * 128:(i + 1) * 128], shnewS[:, g0 + i, :]) for i in range(n_)], identf[:], ["shnewS"], [pk(b)])
                    CP(EV(), sth[0:16, g0 * 128:(g0 + n_) * 128], PS[0:16, b, 0:n_ * 128], [pk(b)], ["sth"])
                mk.dma("sp", o_shift_s.ap(), sth[0:16, :], reads=["sth"], key="o_shift_s")

            if cfg.get("stop") == 4 or stopped:
                mk.barrier()
                continue
            if K == "P" and is_last_p:
                mk.barrier()
                arena.off = mark_seg
                stg = A([8, 128])
                for half in range(2):
                    b = bank()
                    TR([(PS[:, b, i * 128:(i + 1) * 128], hT32[:, (half * 4 + i) * 128:(half * 4 + i + 1) * 128]) for i in range(4)],
                       identf[:], ["hT32"], [pk(b)])
                    CP(EV(), stg[:, half * 4:half * 4 + 4, :], r3(PS[:, b, :]), [pk(b)], ["stg"])
                mk.dma("sp", o_ssm_p.ap().rearrange("(b r) n -> r b n", r=128), stg, reads=["stg"], key="o_ssm_p")
                stw = A([8, 128])
                for half in range(2):
                    b = bank()
                    TR([(PS[0:64, b, i * 128:(i + 1) * 128], S32[:, half * 4 + i, :]) for i in range(4)], identf[:], ["S32"], [pk(b)])
                    CP(EV(), stw[0:64, half * 4:half * 4 + 4, :], r3(PS[0:64, b, :]), [pk(b)], ["stw"])
                mk.dma("sp", o_wkv_p.ap().rearrange("(c j) v k -> v c j k", j=2),
                       stw[0:64, :, :].rearrange("p c (j k) -> p c j k", j=2), reads=["stw"], key="o_wkv_p")
                stc = A([1536])
                for grp in range(3):
                    b = bank()
                    TR([(PS[0:3, b, i * 128:(i + 1) * 128], cvcarry[:, grp * 4 + i, :]) for i in range(4)], identf[:], ["cvcarry"], [pk(b)])
                    CP(EV(), stc[0:3, grp * 512:(grp + 1) * 512], PS[0:3, b, :], [pk(b)], ["stc"])
                mk.dma("sp", o_conv_p.ap(), stc[0:3, :], reads=["stc"], key="o_conv_p")
                sts = A([128])
                b = bank()
                TR([(PS[0:26, b, 0:128], shcarry[:, :])], identf[:], ["shcarry"], [pk(b)])
                CP(EV(), sts[0:26, :], PS[0:26, b, 0:128], [pk(b)], ["sts"])
                mk.dma("sp", o_shift_p.ap(), sts[0:26, :], reads=["sts"], key="o_shift_p")

            if cfg.get("stop") == 5:
                mk.barrier()
                continue
            mk.barrier()
            arena.off = mark_seg
            for half in range(2):
                s0, s1 = wslot(), wslot()
                wload(s0, w_out, 0, 8, half * 512, 512)
                wload(s1, w_out, 1024, 8, half * 512, 512)
                for sub in range(nsub):
                    tsl = slice(sub * 128, (sub + 1) * 128)
                    b = bank()
                    MM([(PS[:, b, :], YT[:, kc, tsl], WS[:, (s0 if kc < 8 else s1), kc % 8, :], kc == 0, kc == 15) for kc in range(16)],
                       ["YT", ("ws", s0), ("ws", s1)], [pk(b)])
                    xh = x_tm[:, sub, half * 512:(half + 1) * 512]
                    TT("dve", xh, xh, PS[:, b, :], ALU.add, [("x", sub), pk(b)], [("x", sub)])
            for sub in range(nsub):
                dbg_tap("hmid_%s_%d" % (segname, sub), x_tm[:, sub, :], [128, 1024], ("x", sub))
            for sub in range(nsub):
                norm_T(x_tm[:, sub, :], ("x", sub), g_ffn, unT, "unT", sub * 128, None)
            actT = A([22, NTK], BF16)
            sgf = [A([NTK]) for _ in range(2)]
            for f0 in range(0, 22, 4):
                nf_ = min(4, 22 - f0)
                sg_, su_ = wslot(), wslot()
                wload(sg_, w_gate, 0, 8, f0 * 128, nf_ * 128)
                wload(su_, w_up, 0, 8, f0 * 128, nf_ * 128)
                for fi in range(nf_):
                    fb = f0 + fi
                    bg = bank()
                    MM([(PS[:, bg, 0:NTK], WS[:, sg_, dk, fi * 128:(fi + 1) * 128], unT[:, dk, :], dk == 0, dk == 7) for dk in range(8)],
                       ["unT", ("ws", sg_)], [pk(bg)])
                    bu = bank()
                    MM([(PS[:, bu, 0:NTK], WS[:, su_, dk, fi * 128:(fi + 1) * 128], unT[:, dk, :], dk == 0, dk == 7) for dk in range(8)],
                       ["unT", ("ws", su_)], [pk(bu)])
                    sgb, sgk = sgf[fb % 2], ("sgf", fb % 2)
                    ACT(sgb, PS[:, bg, 0:NTK], AF.Silu, [pk(bg)], [sgk])
                    TT("dve", actT[:, fb, :], sgb, PS[:, bu, 0:NTK], ALU.mult, [sgk, pk(bu)], ["actT"])
            for half in range(2):
                sl3 = [wslot(), wslot(), wslot()]
                wload(sl3[0], w_down, 0, 8, half * 512, 512)
                wload(sl3[1], w_down, 1024, 8, half * 512, 512)
                wload(sl3[2], w_down, 2048, 6, half * 512, 512)
                for sub in range(nsub):
                    tsl = slice(sub * 128, (sub + 1) * 128)
                    b = bank()
                    MM([(PS[:, b, :], actT[:, fb, tsl], WS[:, sl3[fb // 8], fb % 8, :], fb == 0, fb == 21) for fb in range(22)],
                       ["actT"] + [("ws", q) for q in sl3], [pk(b)])
                    xh = x_tm[:, sub, half * 512:(half + 1) * 512]
                    TT("dve", xh, xh, PS[:, b, :], ALU.add, [("x", sub), pk(b)], [("x", sub)])
            pT = A([2, NTK], BF16)
            ptm = A([256]); ptb = A([256], BF16)
            for sub in range(nsub):
                norm_T(x_tm[:, sub, :], ("x", sub), g_ple, unT, "unT", sub * 128, None)
                mk.dma("sp", ptm, pcat.ap()[row0 + sub * 128:row0 + (sub + 1) * 128, :], writes=["ptm"])
                CP("dve", ptb, ptm, ["ptm"], ["ptb"])
                b = bank()
                TR([(PS16[:, b, i * 128:(i + 1) * 128], ptb[:, i * 128:(i + 1) * 128]) for i in range(2)], identb[:], ["ptb"], [pk(b)])
                CP(EV(), pT[:, :, sub * 128:(sub + 1) * 128], r3(PS16[:, b, 0:256]), [pk(b)], ["pT"])
            gt = A([512])
            for half in range(2):
                s = wslot()
                wload(s, w_plg, 0, 8, half * 512, 512)
                for sub in range(nsub):
                    tsl = slice(sub * 128, (sub + 1) * 128)
                    bg = bank()
                    MM([(PS[:, bg, :], unT[:, dk, tsl], WS[:, s, dk, :], dk == 0, dk == 7) for dk in range(8)], ["unT", ("ws", s)], [pk(bg)])
                    ACT(gt, PS[:, bg, :], AF.Sigmoid, [pk(bg)], ["gt"])
                    bp = bank()
                    MM([(PS[:, bp, :], pT[:, kc, tsl], WPP[:, kc, half * 512:(half + 1) * 512], kc == 0, kc == 1) for kc in range(2)],
                       ["pT", "WPP"], [pk(bp)])
                    TT("dve", gt, gt, PS[:, bp, :], ALU.mult, ["gt", pk(bp)], ["gt"])
                    xh = x_tm[:, sub, half * 512:(half + 1) * 512]
                    TT("dve", xh, xh, gt, ALU.add, [("x", sub), "gt"], [("x", sub)])
            yo = [A([1024]) for _ in range(2)]
            r_gfin = A([1024])
            mk.dma("sp", r_gfin, bass.AP(vec["norm_final"], 0, [[0, 128], [1, 1024]]), writes=["r_gfin"])
            for sub in range(nsub):
                ss, rs = small[:, 0:1], small[:, 1:2]
                ACT(arena_tmp["junk"], x_tm[:, sub, :], AF.Square, [("x", sub)], ["un", "ss"], accum=ss)
                ACT(rs, ss, AF.Ln, ["ss"], ["rs"], scale=1.0 / D, bias=NORM_EPS)
                ACT(rs, rs, AF.Exp, ["rs"], ["rs"], scale=-0.5)
                STT("dve", yo[sub % 2], x_tm[:, sub, :], rs, r_gfin, ALU.mult, ALU.mult, [("x", sub), "rs", "r_gfin"], [("yo", sub % 2)])
                mk.dma("sp", ycat.ap()[row0 + sub * 128:row0 + (sub + 1) * 128, :], yo[sub % 2], reads=[("yo", sub % 2)],
                       key=("yo", sub % 2))
            mk.barrier()

        mk.barrier()
        mk.emit_all()
    return nc, dbg_out


_VEC_NAMES = ["norm_mix", "norm_ffn", "norm_ple", "ssd_norm", "conv_b", "shift_mu", "w0", "a0", "k_k", "k_a", "r_k",
              "ln_x_w", "ln_x_b", "dt_bias", "a_log", "d_skip"]


def make_in_map(inp, c):
    f = lambda a: np.ascontiguousarray(a, dtype=np.float32)
    m = {}
    m["xcat"] = f(np.concatenate([inp["x_prompt"][c], inp["x_sample"][16 * c:16 * c + 16].reshape(128, D)], axis=0))
    m["pcat"] = f(np.concatenate([inp["p_prompt"][0, c], inp["p_sample"][0, 16 * c:16 * c + 16].reshape(128, 256)], axis=0))
    m["st_ssm"] = f(inp["state_ssm"][0, 16 * c:16 * c + 16])
    m["st_conv"] = f(inp["state_conv"][0, 16 * c:16 * c + 16].reshape(48, 1536))
    m["st_wkv"] = f(inp["state_wkv"][0, 16 * c:16 * c + 16])
    m["st_shift"] = f(inp["state_shift"][0, 16 * c:16 * c + 16].reshape(16, 3328))
    for k in ("w_in", "w_out", "w_gate", "w_up", "w_down", "w_ple_gate", "w_ple_proj", "w2", "a2", "g2", "conv_w"):
        m[k] = f(inp[k][0])
    for k in _VEC_NAMES:
        m[k] = f(inp[k][0].reshape(1, -1))
    m["norm_final"] = f(inp["norm_final"].reshape(1, -1))
    return m


_NC_CACHE = {}


def kernel(**inputs):
    if "nc" not in _NC_CACHE:
        _NC_CACHE["nc"] = build()[0]
    nc = _NC_CACHE["nc"]
    in_maps = [make_in_map(inputs, c) for c in range(NCORES)]
    res = run_bass_kernel_spmd(nc, in_maps, core_ids=list(range(NCORES)))
    R = res.results
    y_prompt = np.stack([R[c]["ycat"][:2048] for c in range(NCORES)], axis=0)
    y_sample = np.concatenate([R[c]["ycat"][2048:].reshape(16, 8, D) for c in range(NCORES)], axis=0)
    ssm_p = np.stack([R[c]["ssm_p"].reshape(16, 64, 128) for c in range(NCORES)], axis=0)[None]
    conv_p = np.stack([R[c]["conv_p"] for c in range(NCORES)], axis=0)[None]
    wkv_p = np.stack([R[c]["wkv_p"] for c in range(NCORES)], axis=0)[None]
    shift_p = np.stack([R[c]["shift_p"].reshape(1, 3328) for c in range(NCORES)], axis=0)[None]
    ssm_s = np.concatenate([R[c]["ssm_s"] for c in range(NCORES)], axis=0)[None]
    conv_s = np.concatenate([R[c]["conv_s"].reshape(16, 3, 1536) for c in range(NCORES)], axis=0)[None]
    wkv_s = np.concatenate([R[c]["wkv_s"] for c in range(NCORES)], axis=0)[None]
    shift_s = np.concatenate([R[c]["shift_s"].reshape(16, 1, 3328) for c in range(NCORES)], axis=0)[None]
    outs = (y_prompt, y_sample, ssm_p, conv_p, wkv_p, shift_p, ssm_s, conv_s, wkv_s, shift_s)
    return tuple(np.ascontiguousarray(o, dtype=np.float32) for o in outs)
```

```python
import numpy as np
import concourse.bass as bass
import concourse.mybir as mybir
from concourse.bass_utils import run_bass_kernel_spmd
from contextlib import ExitStack

F32 = mybir.dt.float32
BF16 = mybir.dt.bfloat16
AF = mybir.ActivationFunctionType
ALU = mybir.AluOpType
AX = mybir.AxisListType

NCORES = 8
D = 1024
INP = 5904
DFF = 2816
C_Z, C_XBC, C_DT, C_RW = 0, 1024, 2560, 2576
CEXP = float(np.exp(-0.5))
NORM_EPS = 1e-6
GN_EPS = 64e-5
NTOK = 2048 + 128


class MK:
    ENG = ("pe", "act", "dve", "pool", "sp")

    def __init__(self, nc, es):
        self.nc = nc
        self.es = es
        self.ops = {e: [] for e in self.ENG}
        self.sem = {e: es.enter_context(nc.semaphore("s_" + e)) for e in self.ENG}
        self.cnt = {id(s): 0 for s in self.sem.values()}
        self.dsem = {}
        self.res = {}
        self.waited = {e: {} for e in self.ENG}
        self.allsems = {id(s): s for s in self.sem.values()}
        self.ninst = 0

    def sb(self, name, shape, dt=F32):
        return self.es.enter_context(self.nc.sbuf_tensor(name, list(shape), dt))

    def ps(self, name, shape, dt=F32):
        return self.es.enter_context(self.nc.psum_tensor(name, list(shape), dt))

    def _dma_sem(self, key):
        s = self.dsem.get(key)
        if s is None:
            s = self.es.enter_context(self.nc.semaphore("d%d" % len(self.dsem)))
            self.dsem[key] = s
            self.cnt[id(s)] = 0
            self.allsems[id(s)] = s
        return s

    def _deps(self, eng, reads, writes):
        w = {}

        def need(p):
            if p is None:
                return
            s, v = p
            if w.get(s, 0) < v:
                w[s] = v

        for r in reads:
            st = self.res.get(r)
            if st:
                need(st[0])
        for k in writes:
            st = self.res.get(k)
            if st:
                need(st[0])
                for s, v in st[1].items():
                    need((s, v))
        out = []
        for s, v in w.items():
            if self.waited[eng].get(s, 0) >= v:
                continue
            self.waited[eng][s] = v
            out.append((self.allsems[s], v))
        return out

    def _commit(self, reads, writes, sem, val):
        for r in reads:
            st = self.res.setdefault(r, [None, {}])
            st[1][id(sem)] = val
        for k in writes:
            self.res[k] = [(id(sem), val), {}]

    def op(self, eng, fn, reads=(), writes=()):
        waits = self._deps(eng, reads, writes)
        sem = self.sem[eng]
        val = self.cnt[id(sem)] + 1
        self.cnt[id(sem)] = val
        self._commit(reads, writes, sem, val)

        def emit(h, waits=waits, fn=fn, sem=sem):
            for s, v in waits:
                h.wait_ge(s, v)
            ins = fn(h)
            ins.then_inc(sem, 1)

        self.ops[eng].append(emit)
        self.ninst += 1

    def dma(self, eng, out_ap, in_ap, reads=(), writes=(), key=None, **kw):
        if key is None:
            key = writes[0] if writes else reads[0]
        sem = self._dma_sem(key)
        waits = self._deps(eng, reads, writes)
        val = self.cnt[id(sem)] + 16
        self.cnt[id(sem)] = val
        self._commit(reads, writes, sem, val)

        def emit(h, waits=waits, sem=sem):
            for s, v in waits:
                h.wait_ge(s, v)
            h.dma_start(out=out_ap, in_=in_ap, **kw).then_inc(sem, 16)

        self.ops[eng].append(emit)

    def barrier(self):
        finals = [(i, s, self.cnt[i]) for i, s in self.allsems.items() if self.cnt[i] > 0]
        for e in self.ENG:
            todo = []
            for i, s, v in finals:
                if self.waited[e].get(i, 0) < v:
                    self.waited[e][i] = v
                    todo.append((s, v))

            def emit(h, todo=todo):
                for s, v in todo:
                    h.wait_ge(s, v)

            self.ops[e].append(emit)
        self.res = {}

    def emit_all(self):
        nc = self.nc
        with nc.Block() as block:
            @block.tensor
            def _(h):
                for f in self.ops["pe"]:
                    f(h)

            @block.scalar
            def _(h):
                for f in self.ops["act"]:
                    f(h)

            @block.vector
            def _(h):
                for f in self.ops["dve"]:
                    f(h)

            @block.gpsimd
            def _(h):
                for f in self.ops["pool"]:
                    f(h)

            @block.sync
            def _(h):
                for f in self.ops["sp"]:
                    f(h)


def _prod(s):
    n = 1
    for x in s:
        n *= x
    return n


def _nd(ap, shape):
    if len(shape) == 1:
        return ap
    names = ["d%d" % i for i in range(len(shape))]
    kw = {names[i]: shape[i] for i in range(1, len(shape))}
    return ap.rearrange("p (%s) -> p %s" % (" ".join(names), " ".join(names)), **kw)


class Arena:
    def __init__(self, mk, name, words):
        self.t32 = mk.sb(name, [128, words], F32)
        self.t16 = self.t32.bitcast(BF16)
        self.words = words
        self.off = 0
        self.peak = 0

    def reset(self):
        self.off = 0

    def alloc(self, shape, dt=F32):
        n = _prod(shape)
        if dt == F32:
            ap = self.t32[:, self.off:self.off + n]
            self.off += n
        else:
            ap = self.t16[:, 2 * self.off:2 * self.off + n]
            self.off += (n + 1) // 2
        assert self.off <= self.words, ("arena overflow", self.off, self.words)
        self.peak = max(self.peak, self.off)
        return _nd(ap, list(shape))


def build(cfg=None):
    cfg = cfg or {}
    segs_cfg = cfg.get("segs", ["P0", "P1", "P2", "P3", "S"])
    dbg = cfg.get("dbg", ())

    nc = bass.Bass("TRN2", target_bir_lowering=False)

    def din(name, shape):
        return nc.dram_tensor(name, list(shape), F32, kind="ExternalInput")

    def dout(name, shape):
        return nc.dram_tensor(name, list(shape), F32, kind="ExternalOutput")

    xcat = din("xcat", [NTOK, D])
    pcat = din("pcat", [NTOK, 256])
    st_ssm = din("st_ssm", [16, 16, 64, 128])
    st_conv = din("st_conv", [48, 1536])
    st_wkv = din("st_wkv", [16, 16, 64, 64])
    st_shift = din("st_shift", [16, 3328])
    w_in = din("w_in", [D, INP])
    w_out = din("w_out", [2048, D])
    w_gate = din("w_gate", [D, DFF])
    w_up = din("w_up", [D, DFF])
    w_down = din("w_down", [DFF, D])
    w_plg = din("w_ple_gate", [D, D])
    w_plp = din("w_ple_proj", [256, D])
    w2 = din("w2", [64, D])
    a2 = din("a2", [64, D])
    g2 = din("g2", [128, D])
    vec = {}
    for nm, n in [("norm_mix", D), ("norm_ffn", D), ("norm_ple", D), ("norm_final", D), ("ssd_norm", D),
                  ("conv_b", 1536), ("shift_mu", 3328), ("w0", D), ("a0", D), ("k_k", D), ("k_a", D), ("r_k", D),
                  ("ln_x_w", D), ("ln_x_b", D), ("dt_bias", 16), ("a_log", 16), ("d_skip", 16)]:
        vec[nm] = din(nm, [1, n])
    conv_w = din("conv_w", [4, 1536])

    ycat = dout("ycat", [NTOK, D])
    o_ssm_p = dout("ssm_p", [1024, 128])
    o_conv_p = dout("conv_p", [3, 1536])
    o_wkv_p = dout("wkv_p", [16, 64, 64])
    o_shift_p = dout("shift_p", [26, 128])
    o_ssm_s = dout("ssm_s", [16, 16, 64, 128])
    o_conv_s = dout("conv_s", [48, 1536])
    o_wkv_s = dout("wkv_s", [16, 16, 64, 64])
    o_shift_s = dout("shift_s", [16, 3328])
    dbg_out = {}

    with ExitStack() as es:
        mk = MK(nc, es)
        PSh = mk.ps("PS", [128, 8, 512], F32)
        PS = PSh
        PS16 = PSh.bitcast(BF16)

        def TT(E, out, in0, in1, op, r, w):
            mk.op(E, lambda h: h.tensor_tensor(out=out, in0=in0, in1=in1, op=op), r, w)

        def TS(E, out, in0, s1, op0, r, w, s2=None, op1=None):
            if op1 is None:
                mk.op(E, lambda h: h.tensor_scalar(out=out, in0=in0, scalar1=s1, scalar2=None, op0=op0), r, w)
            else:
                mk.op(E, lambda h: h.tensor_scalar(out=out, in0=in0, scalar1=s1, scalar2=s2, op0=op0, op1=op1), r, w)

        def STT(E, out, in0, sc, in1, op0, op1, r, w):
            mk.op(E, lambda h: h.scalar_tensor_tensor(out=out, in0=in0, scalar=sc, in1=in1, op0=op0, op1=op1), r, w)

        def ACT(out, in_, func, r, w, bias=None, scale=None, accum=None):
            kw = {}
            if bias is not None:
                kw["bias"] = bias
            if scale is not None:
                kw["scale"] = scale
            if accum is not None:
                kw["accum_out"] = accum
            mk.op("act", lambda h: h.activation(out=out, in_=in_, func=func, **kw), r, w)

        def CP(E, out, in_, r, w):
            if E == "act":
                mk.op(E, lambda h: h.copy(out=out, in_=in_), r, w)
            else:
                mk.op(E, lambda h: h.tensor_copy(out=out, in_=in_), r, w)

        def MM(items, r, w):
            groups, cur, last_bp = [], [], None
            for it_ in items:
                bp = it_[1].base_partition()
                if cur and bp != last_bp:
                    groups.append(cur)
                    cur = []
                cur.append(it_)
                last_bp = bp
            groups.append(cur)
            for gi in groups:
                def f(h, gi=gi):
                    ins = None
                    for (o, l, rr, st, sp) in gi:
                        ins = h.matmul(o, lhsT=l, rhs=rr, start=st, stop=sp)
                    return ins
                mk.op("pe", f, r, w)

        def TR(items, ident, r, w):
            def f(h):
                ins = None
                for (o, i) in items:
                    ins = h.transpose(out=o, in_=i, identity=ident)
                return ins
            mk.op("pe", f, r, w)

        def SCAN(out, d0, d1, r, w):
            mk.op("dve", lambda h: h.tensor_tensor_scan(out=out, data0=d0, data1=d1, initial=0.0, op0=ALU.mult, op1=ALU.add), r, w)

        def RED(E, out, in_, r, w):
            mk.op(E, lambda h: h.tensor_reduce(out=out, in_=in_, axis=AX.X, op=ALU.add), r, w)

        def MEMSET(E, ap, val, r, w):
            mk.op(E, lambda h: h.memset(ap, val), r, w)

        def ASEL(out, in_, pattern, cmp, fill, base, cm, r, w):
            mk.op("pool", lambda h: h.affine_select(out=out, in_=in_, pattern=pattern, compare_op=cmp, fill=fill,
                                                    base=base, channel_multiplier=cm), r, w)

        _evac = [0]

        def EV():
            _evac[0] += 1
            return "act" if _evac[0] % 2 else "dve"

        free_banks = list(range(8))
        rot = [0]

        def bank():
            b = free_banks[rot[0] % len(free_banks)]
            rot[0] += 1
            return b

        def hold(n):
            bs = [free_banks.pop(0) for _ in range(n)]
            return bs

        def release(bs):
            for b in bs:
                free_banks.append(b)
            free_banks.sort()

        def pk(b):
            return ("ps", b)

        def bc(ap, shape):
            return ap.to_broadcast(list(shape))

        def dbg_tap(name, ap, shape, key):
            if name not in dbg:
                return
            d = dout("dbg_" + name, shape)
            dbg_out[name] = d
            mk.dma("sp", d.ap(), ap, reads=[key], key=("dbg", name))

        identf = mk.sb("identf", [128, 128])
        identb = mk.sb("identb", [128, 128], BF16)
        blkones = mk.sb("blkones", [128, 128])
        mLE = {k: mk.sb("mLE" + k, [128, 128]) for k in "PS"}
        mGT = {k: mk.sb("mGT" + k, [128, 128]) for k in "PS"}
        mON = {k: mk.sb("mON" + k, [128, 128]) for k in "PS"}
        mAM = {k: mk.sb("mAM" + k, [128, 512], BF16) for k in "PS"}
        mLEb = {k: mk.sb("mLEb" + k, [128, 128], BF16) for k in "PS"}
        mGTb = {k: mk.sb("mGTb" + k, [128, 128], BF16) for k in "PS"}
        rmask = {"P": mk.sb("rmP", [128, 512]), "S": mk.sb("rmS", [128, 128])}
        seqcol = mk.sb("seqcol", [128, 16, 128], BF16)
        seqrow = mk.sb("seqrow", [128, 16])
        CONSTS = ["identf", "identb", "blkones", "masks", "vecs"]

        def c128(name, nblk):
            return mk.sb("c_" + name, [128, nblk])

        g_mix, g_ffn, g_ple, g_ssdn = c128("g_mix", 8), c128("g_ffn", 8), c128("g_ple", 8), c128("g_ssdn", 8)
        c_cb, c_mu = c128("conv_b", 12), c128("mu", 26)
        c_w0, c_a0, c_kk, c_ka, c_rk, c_lnw, c_lnb = [c128(n, 8) for n in ("w0", "a0", "kk", "ka", "rk", "lnw", "lnb")]
        c_cw = mk.sb("c_cw", [128, 12, 4])
        r_dtb, r_ah, r_dsk = mk.sb("r_dtb", [128, 16]), mk.sb("r_ah", [128, 16]), mk.sb("r_dsk", [128, 16])
        W2A = mk.sb("W2A", [128, 1024], BF16)
        G2 = mk.sb("G2", [128, 1024], BF16)
        WPP = mk.sb("WPP", [128, 2, 1024], BF16)
        NSLOT = 4
        WS = mk.sb("WS", [128, NSLOT, 8, 512], BF16)
        shcarry = mk.sb("shcarry", [128, 26])
        cvcarry = mk.sb("cvcarry", [128, 12, 3])
        hT32 = mk.sb("hT32", [128, 1024])
        hTb = mk.sb("hTb", [128, 1024], BF16)
        S32 = mk.sb("S32", [128, 8, 64])
        Sb = mk.sb("Sb", [128, 8, 2, 64], BF16)
        small = mk.sb("small", [128, 64])
        arena = Arena(mk, "arena", 36000)

        def vload(dst, name, nblk):
            mk.dma("sp", dst[:], bass.AP(vec[name], 0, [[1, 128], [128, nblk]]), writes=["vecs"], key="vecs",
                   allow_slow_non_contiguous=True)

        vload(g_mix, "norm_mix", 8); vload(g_ffn, "norm_ffn", 8); vload(g_ple, "norm_ple", 8); vload(g_ssdn, "ssd_norm", 8)
        vload(c_cb, "conv_b", 12); vload(c_mu, "shift_mu", 26)
        vload(c_w0, "w0", 8); vload(c_a0, "a0", 8); vload(c_kk, "k_k", 8); vload(c_ka, "k_a", 8); vload(c_rk, "r_k", 8)
        vload(c_lnw, "ln_x_w", 8); vload(c_lnb, "ln_x_b", 8)
        for j in range(4):
            mk.dma("sp", c_cw[:, :, j], bass.AP(conv_w, j * 1536, [[1, 128], [128, 12]]), writes=["vecs"], key="vecs",
                   allow_slow_non_contiguous=True)
        for dst, nm in ((r_dtb, "dt_bias"), (r_ah, "a_log"), (r_dsk, "d_skip")):
            mk.dma("sp", dst[:], bass.AP(vec[nm], 0, [[0, 128], [1, 16]]), writes=["vecs"], key="vecs")
        ACT(r_ah[:], r_ah[:], AF.Exp, ["vecs"], ["vecs"])
        TS("dve", r_ah[:], r_ah[:], -1.0, ALU.mult, ["vecs"], ["vecs"])
        mk.dma("pool", W2A[0:64, :], w2.ap(), writes=["W2A"])
        mk.dma("pool", W2A[64:128, :], a2.ap(), writes=["W2A"])
        mk.dma("pool", G2[:], g2.ap(), writes=["G2"])
        mk.dma("pool", WPP[:], w_plp.ap().rearrange("(k p) n -> p k n", p=128), writes=["WPP"])

        M = ["masks"]
        MEMSET("pool", identf[:], 0.0, [], M)
        ASEL(identf[:], identf[:], [[-1, 128]], ALU.not_equal, 1.0, 0, 1, M, M)
        CP("pool", identb[:], identf[:], M, M)
        MEMSET("pool", blkones[:], 1.0, [], M)
        bo3 = blkones[:].rearrange("p (c l) -> p c l", l=64)
        ASEL(bo3, bo3, [[-64, 2], [0, 64]], ALU.is_ge, 0.0, 0, 1, M, M)
        ASEL(bo3, bo3, [[64, 2], [0, 64]], ALU.is_ge, 0.0, 63, -1, M, M)
        for k in "PS":
            MEMSET("pool", mLE[k][:], 1.0, [], M)
            ASEL(mLE[k][:], mLE[k][:], [[1, 128]], ALU.is_ge, 0.0, 0, -1, M, M)
            MEMSET("pool", mGT[k][:], 1.0, [], M)
            ASEL(mGT[k][:], mGT[k][:], [[-1, 128]], ALU.is_ge, 0.0, -1, 1, M, M)
            MEMSET("pool", mON[k][:], 1.0, [], M)
            if k == "S":
                for t_ in (mLE[k], mGT[k], mON[k]):
                    v3 = t_[:].rearrange("p (c l) -> p c l", l=8)
                    ASEL(v3, v3, [[-8, 16], [0, 8]], ALU.is_ge, 0.0, 0, 1, M, M)
                    ASEL(v3, v3, [[8, 16], [0, 8]], ALU.is_ge, 0.0, 7, -1, M, M)
            CP("pool", mLEb[k][:], mLE[k][:], M, M)
            CP("pool", mGTb[k][:], mGT[k][:], M, M)
            for q in (1, 3):
                CP("pool", mAM[k][:, q * 128:(q + 1) * 128], mLE[k][:], M, M)
            for q in (0, 2):
                TT("pool", mAM[k][:, q * 128:(q + 1) * 128], mLE[k][:], identf[:], ALU.subtract, M, M)
        MEMSET("pool", rmask["P"][:], 1.0, [], M)
        MEMSET("pool", rmask["P"][:].rearrange("p (c l) -> p c l", l=128)[:, :, 0:1], 0.0, M, M)
        MEMSET("pool", rmask["S"][:], 1.0, [], M)
        MEMSET("pool", rmask["S"][:].rearrange("p (c l) -> p c l", l=8)[:, :, 0:1], 0.0, M, M)
        MEMSET("pool", seqcol[:], 1.0, [], M)
        sc4 = seqcol[:].rearrange("p s (c l) -> p s c l", l=8)
        ASEL(sc4, sc4, [[-1, 16], [1, 16], [0, 8]], ALU.is_ge, 0.0, 0, 0, M, M)
        ASEL(sc4, sc4, [[1, 16], [-1, 16], [0, 8]], ALU.is_ge, 0.0, 0, 0, M, M)
        MEMSET("pool", seqrow[:], 1.0, [], M)
        ASEL(seqrow[:], seqrow[:], [[-8, 16]], ALU.is_ge, 0.0, 0, 1, M, M)
        ASEL(seqrow[:], seqrow[:], [[8, 16]], ALU.is_ge, 0.0, 7, -1, M, M)
        MEMSET("pool", shcarry[:], 0.0, [], ["shcarry"])
        MEMSET("pool", cvcarry[:], 0.0, [], ["cvcarry"])
        MEMSET("pool", hT32[:], 0.0, [], ["hT32"])
        MEMSET("pool", hTb[:], 0.0, [], ["hTb"])
        MEMSET("pool", S32[:], 0.0, [], ["S32"])
        MEMSET("pool", Sb[:], 0.0, [], ["Sb"])
        mk.barrier()

        slot_i = [0]

        def wslot():
            s = slot_i[0] % NSLOT
            slot_i[0] += 1
            return s

        def wload(s, dram, r0, nk, c0, ncol, col_off=0):
            src = dram.ap()[r0:r0 + nk * 128, c0:c0 + ncol].rearrange("(k p) n -> p k n", p=128)
            mk.dma("pool", WS[:, s, 0:nk, col_off:col_off + ncol], src, writes=[("ws", s)])

        def norm_T(src, src_key, gains, dstT, dst_key, col0, tmpk):
            junk = arena_tmp["junk"]
            un = arena_tmp["un"]
            ss, rs = small[:, 0:1], small[:, 1:2]
            ACT(junk, src, AF.Square, [src_key], ["un", "ss"], accum=ss)
            ACT(rs, ss, AF.Ln, ["ss"], ["rs"], scale=1.0 / D, bias=NORM_EPS)
            ACT(rs, rs, AF.Exp, ["rs"], ["rs"], scale=-0.5)
            TS("dve", un, src, rs, ALU.mult, [src_key, "rs"], ["un"])
            for half in range(2):
                b = bank()
                TR([(PS16[:, b, j * 128:(j + 1) * 128], un[:, (half * 4 + j) * 128:(half * 4 + j + 1) * 128]) for j in range(4)],
                   identb[:], ["un"], [pk(b)])
                TT(EV() if False else "dve", dstT[:, half * 4:half * 4 + 4, col0:col0 + 128],
                   PS16[:, b, 0:512].rearrange("p (a t) -> p a t", t=128),
                   bc(gains[:, half * 4:half * 4 + 4].unsqueeze(2), [128, 4, 128]), ALU.mult, [pk(b)], [dst_key])

        arena_tmp = {}

        SEGS = {"P0": ("P", 0, 4), "P1": ("P", 512, 4), "P2": ("P", 1024, 4), "P3": ("P", 1536, 4), "S": ("S", 2048, 1)}
        last_prompt = [s for s in segs_cfg if s.startswith("P")][-1] if any(s.startswith("P") for s in segs_cfg) else None

        for segname in segs_cfg:
            kind, row0, nsub = SEGS[segname]
            K = kind
            NTK = nsub * 128
            nch, Lc = (1, NTK) if kind == "P" else (16, 8)
            ncq, Lq = (nsub, 128) if kind == "P" else (16, 8)
            is_last_p = (segname == last_prompt)
            arena.reset()
            A = arena.alloc
            arena_tmp["un"] = A([1024], BF16)
            arena_tmp["junk"] = arena_tmp["un"]
            x_tm = A([nsub, 1024])
            unT = A([8, NTK], BF16)
            dt_tm = A([nsub, 16]); dtA = A([nsub, 16])
            LoT = A([NTK], BF16); sgT = A([NTK], BF16)
            YT = A([16, NTK], BF16)
            rawb = [A([nch, 1 + Lc]) for _ in range(2)]
            tsd = [A([NTK]) for _ in range(2)]
            if kind == "S":
                stshT = A([26, 16]); stcvT = A([12, 16, 3])
                shnewS = A([26, 16]); cnewS = A([12, 16, 3])
            mark_wkv = arena.off
            zs = A([nsub, 1024], BF16)
            xs_tm = A([nsub, 1024], BF16)
            B_tm = A([nsub, 2, 128], BF16)
            BT = A([2, NTK], BF16); CT = A([2, NTK], BF16)
            mark_seg = arena.off
            h3 = lambda ap, a=16: ap.rearrange("p (a x) -> p a x", a=a)
            r3 = lambda ap: ap.rearrange("p (a t) -> p a t", t=128)

            def seg3(ap2d):
                return ap2d.rearrange("p (c l) -> p c l", l=Lc)

            for sub in range(nsub):
                mk.dma("sp", x_tm[:, sub, :], xcat.ap()[row0 + sub * 128:row0 + (sub + 1) * 128, :], writes=[("x", sub)])
                norm_T(x_tm[:, sub, :], ("x", sub), g_mix, unT, "unT", sub * 128, None)

            if kind == "S":
                sst = A([3328]); scv = A([1536])
                mk.dma("sp", sst[0:16, :], st_shift.ap(), writes=["sst"])
                mk.dma("sp", scv[0:48, :], st_conv.ap(), writes=["scv"])
                for g0 in range(0, 26, 13):
                    b = bank()
                    TR([(PS[:, b, (i - g0) * 16:(i - g0) * 16 + 16], sst[0:16, i * 128:(i + 1) * 128]) for i in range(g0, g0 + 13)],
                       identf[0:16, 0:16], ["sst"], [pk(b)])
                    CP(EV(), stshT[:, g0:g0 + 13, :], PS[:, b, 0:13 * 16].rearrange("p (a s) -> p a s", s=16), [pk(b)], ["stshT"])
                for g0 in range(0, 12, 6):
                    b = bank()
                    TR([(PS[:, b, (i - g0) * 48:(i - g0) * 48 + 48], scv[0:48, i * 128:(i + 1) * 128]) for i in range(g0, g0 + 6)],
                       identf[0:48, 0:48], ["scv"], [pk(b)])
                    CP(EV(), stcvT[:, g0:g0 + 6, :, :], PS[:, b, 0:6 * 48].rearrange("p (a s j) -> p a s j", s=16, j=3),
                       [pk(b)], ["stcvT"])

            if cfg.get("stop") == 1:
                mk.barrier()
                continue
            s = wslot()
            wload(s, w_in, 0, 8, C_DT, 16)
            for sub in range(nsub):
                b = bank()
                MM([(PS[:, b, 0:16], unT[:, dk, sub * 128:(sub + 1) * 128], WS[:, s, dk, 0:16], dk == 0, dk == 7) for dk in range(8)],
                   ["unT", ("ws", s)], [pk(b)])
                t16 = small[:, 16:32]
                TT("dve", t16, PS[:, b, 0:16], r_dtb[:], ALU.add, [pk(b)], ["t16"])
                ACT(t16, t16, AF.Exp, ["t16"], ["t16"])
                ACT(dt_tm[:, sub, :], t16, AF.Ln, ["t16"], ["dt"], bias=1.0)
                TT("dve", dtA[:, sub, :], dt_tm[:, sub, :], r_ah[:], ALU.mult, ["dt"], ["dtA"])
            for half in range(2):
                s = wslot()
                wload(s, w_in, 0, 8, C_Z + half * 512, 512)
                for sub in range(nsub):
                    b = bank()
                    MM([(PS[:, b, :], unT[:, dk, sub * 128:(sub + 1) * 128], WS[:, s, dk, :], dk == 0, dk == 7) for dk in range(8)],
                       ["unT", ("ws", s)], [pk(b)])
                    ACT(zs[:, sub, half * 512:(half + 1) * 512], PS[:, b, :], AF.Silu, [pk(b)], ["zs"])

            craw = [A([nch, 3 + Lc]) for _ in range(2)]
            cacc = [A([NTK]) for _ in range(2)]
            xsT = [A([NTK], BF16) for _ in range(2)]
            for grp in range(3):
                s = wslot()
                wload(s, w_in, 0, 8, C_XBC + grp * 512, 512)
                for bi in range(4):
                    blk = grp * 4 + bi
                    cr, ca, ck = craw[blk % 2], cacc[blk % 2], ("craw", blk % 2)
                    b = bank()
                    MM([(PS[:, b, 0:NTK], WS[:, s, dk, bi * 128:(bi + 1) * 128], unT[:, dk, :], dk == 0, dk == 7) for dk in range(8)],
                       ["unT", ("ws", s)], [pk(b)])
                    CP("act", cr[:, :, 3:3 + Lc], seg3(PS[:, b, 0:NTK]), [pk(b)], [ck])
                    if kind == "P":
                        CP("dve", cr[:, 0, 0:3], cvcarry[:, blk, :], ["cvcarry"], [ck])
                        CP("dve", cvcarry[:, blk, :], cr[:, 0, Lc:Lc + 3], [ck], ["cvcarry"])
                    else:
                        CP("dve", cr[:, :, 0:3], stcvT[:, blk, :, :], ["stcvT"], [ck])
                        CP("dve", cnewS[:, blk, :, :], cr[:, :, Lc:Lc + 3], [ck], ["cnewS"])
                    ak = ("cacc", blk % 2)
                    ca3 = seg3(ca)
                    TS("dve", ca3, cr[:, :, 0:Lc], c_cw[:, blk, 0:1], ALU.mult, [ck], [ak])
                    for j in range(1, 4):
                        STT("dve", ca3, cr[:, :, j:j + Lc], c_cw[:, blk, j:j + 1], ca3, ALU.mult, ALU.add, [ck, ak], [ak])
                    if blk < 8:
                        xt, xk = xsT[blk % 2], ("xsT", blk % 2)
                        ACT(xt, ca, AF.Silu, [ak], [xk], bias=c_cb[:, blk:blk + 1])
                        b2 = bank()
                        TR([(PS16[:, b2, i * 128:(i + 1) * 128], xt[:, i * 128:(i + 1) * 128]) for i in range(nsub)],
                           identb[:], [xk], [pk(b2)])
                        CP(EV(), xs_tm[:, :, blk * 128:(blk + 1) * 128], r3(PS16[:, b2, 0:nsub * 128]), [pk(b2)], ["xs_tm"])
                    elif blk < 10:
                        g = blk - 8
                        ACT(BT[:, g, :], ca, AF.Silu, [ak], ["BT"], bias=c_cb[:, blk:blk + 1])
                        b2 = bank()
                        TR([(PS16[:, b2, i * 128:(i + 1) * 128], BT[:, g, i * 128:(i + 1) * 128]) for i in range(nsub)],
                           identb[:], ["BT"], [pk(b2)])
                        CP(EV(), B_tm[:, :, g, :], r3(PS16[:, b2, 0:nsub * 128]), [pk(b2)], ["B_tm"])
                    else:
                        g = blk - 10
                        ACT(CT[:, g, :], ca, AF.Silu, [ak], ["CT"], bias=c_cb[:, blk:blk + 1])

            tsi = [0]

            def tshift(b, rwblk, out_u, out_key):
                i = tsi[0] % 2
                tsi[0] += 1
                rb, td, rk_, tk_ = rawb[i], tsd[i], ("rawb", i), ("tsd", i)
                CP("act", rb[:, :, 1:1 + Lc], seg3(PS[:, b, 0:NTK]), [pk(b)], [rk_])
                if kind == "P":
                    CP("dve", rb[:, 0, 0:1], shcarry[:, rwblk:rwblk + 1], ["shcarry"], [rk_])
                    CP("dve", shcarry[:, rwblk:rwblk + 1], rb[:, 0, Lc:Lc + 1], [rk_], ["shcarry"])
                else:
                    CP("dve", rb[:, :, 0:1], stshT[:, rwblk, :].unsqueeze(2), ["stshT"], [rk_])
                    CP("dve", shnewS[:, rwblk, :].unsqueeze(2), rb[:, :, Lc:Lc + 1], [rk_], ["shnewS"])
                TT("dve", seg3(td), rb[:, :, 0:Lc], rb[:, :, 1:1 + Lc], ALU.subtract, [rk_], [tk_])
                STT("dve", seg3(out_u), seg3(td), c_mu[:, rwblk:rwblk + 1], rb[:, :, 1:1 + Lc], ALU.mult, ALU.add,
                    [tk_, rk_], [out_key])

            s = wslot()
            wload(s, w_in, 0, 8, C_RW + 3072, 256)
            ulo = A([NTK])
            for bi in range(2):
                b = bank()
                MM([(PS[:, b, 0:NTK], WS[:, s, dk, bi * 128:(bi + 1) * 128], unT[:, dk, :], dk == 0, dk == 7) for dk in range(8)],
                   ["unT", ("ws", s)], [pk(b)])
                tshift(b, 24 + bi, ulo, "ulo")
                if bi == 0:
                    ACT(LoT[0:64, :], ulo[0:64, :], AF.Tanh, ["ulo"], ["LoT"])
                    CP("dve", LoT[64:128, :], ulo[64:128, :], ["ulo"], ["LoT"])
                else:
                    ACT(sgT, ulo, AF.Sigmoid, ["ulo"], ["sgT"])

            if cfg.get("stop") == 2:
                mk.barrier()
                continue
            mk.barrier()
            arena.off = mark_seg
            ea, coef, cdd = A([16]), A([16]), A([16])
            CBm = A([2, 128], BF16)
            L1s = [A([4, 128]) for _ in range(2)]; Ebs = [A([4, 128]) for _ in range(2)]; WTs = [A([4, 128], BF16) for _ in range(2)]
            yv = A([1024]); t2 = A([1024]); yn = A([1024], BF16); xpp = A([1024], BF16)
            junk32 = A([1024])
            ss2, rs2 = small[:, 2:4], small[:, 4:6]
            for sub in range(nsub):
                tsl = slice(sub * 128, (sub + 1) * 128)
                b = bank()
                MM([(PS[:, b, 0:16], mLE[K][:], dtA[:, sub, :], True, True),
                    (PS[:, b, 16:32], mGT[K][:], dtA[:, sub, :], True, True),
                    (PS[:, b, 32:48], mON[K][:], dtA[:, sub, :], True, True)], ["dtA"], [pk(b)])
                ACT(ea, PS[:, b, 0:16], AF.Exp, [pk(b)], ["ea"])
                ACT(coef, PS[:, b, 16:32], AF.Exp, [pk(b)], ["coef"])
                ACT(cdd, PS[:, b, 32:48], AF.Exp, [pk(b)], ["cdd"])
                TT("dve", coef, coef, dt_tm[:, sub, :], ALU.mult, ["coef", "dt"], ["coef"])
                b = bank()
                MM([(PS[:, b, g * 128:(g + 1) * 128], BT[:, g, tsl], CT[:, g, tsl], True, True) for g in range(2)],
                   ["BT", "CT"], [pk(b)])
                TT("dve", CBm, r3(PS[:, b, 0:256]), bc(mLEb[K][:].unsqueeze(1), [128, 2, 128]), ALU.mult, [pk(b)], ["CBm"])
                yb = hold(2)
                for hg in range(4):
                    L1, Eb, WT = L1s[hg % 2], Ebs[hg % 2], WTs[hg % 2]
                    kL1, kEb, kWT = ("L1", hg % 2), ("Eb", hg % 2), ("WT", hg % 2)
                    for i in range(4):
                        h_ = hg * 4 + i
                        TS("dve", L1[:, i, :], mGT[K][:], dtA[:, sub, h_:h_ + 1], ALU.mult, ["dtA"], [kL1])
                    b = bank()
                    MM([(PS[:, b, i * 128:(i + 1) * 128], L1[:, i, :], mLE[K][:], True, True) for i in range(4)], [kL1], [pk(b)])
                    ACT(Eb.rearrange("p a t -> p (a t)"), PS[:, b, :], AF.Exp, [pk(b)], [kEb])
                    TT("dve", Eb, Eb, bc(dt_tm[:, sub, hg * 4:hg * 4 + 4].unsqueeze(2), [128, 4, 128]), ALU.mult, [kEb, "dt"], [kEb])
                    g = hg // 2
                    TT("dve", WT, Eb, bc(CBm[:, g, :].unsqueeze(1), [128, 4, 128]), ALU.mult, [kEb, "CBm"], [kWT])
                    MM([(PS[:, yb[g], ((hg * 4 + i) % 8) * 64:((hg * 4 + i) % 8) * 64 + 64], WT[:, i, :],
                         xs_tm[:, sub, (hg * 4 + i) * 64:(hg * 4 + i + 1) * 64], True, True) for i in range(4)],
                       [kWT, "xs_tm"], [("yb", g, hg % 2)])
                TT("dve", h3(xpp), h3(xs_tm[:, sub, :]), bc(coef.unsqueeze(2), [128, 16, 64]), ALU.mult, ["xs_tm", "coef"], ["xpp"])
                ob = hold(2)
                if K == "P":
                    for g in range(2):
                        MM([(PS[:, ob[g], :], CT[:, g, tsl], hTb[:, g * 512:(g + 1) * 512], True, True)], ["CT", "hTb"], [pk(ob[g])])
                else:
                    rhsj = A([2, 16, 8]); cdP = A([16, 8])
                    S0n = [A([8, 128]) for _ in range(2)]
                    h0T = A([8, 128], BF16); CTm = A([2, 128], BF16); xppm = A([1024], BF16)
                    Snew = [A([8, 128]) for _ in range(2)]
                    dq = dtA[:, 0, :].rearrange("p (q j) -> p q j", j=2)
                    for j in range(2):
                        TT("dve", rhsj[:, j, :, :], bc(dq[:, :, j].unsqueeze(1), [128, 16, 8]), bc(seqrow[:, :].unsqueeze(2), [128, 16, 8]),
                           ALU.mult, ["dtA"], ["rhsj"])
                    b = bank()
                    for j in range(2):
                        MM([(PS[j * 64:(j + 1) * 64, b, 0:128], mON["P"][:, 0:64], rhsj[:, j, :, :].rearrange("p s q -> p (s q)"), True, True)],
                           ["rhsj"], [pk(b)])
                    ACT(cdP.rearrange("p s q -> p (s q)"), PS[:, b, 0:128], AF.Exp, [pk(b)], ["cdP"])
                    for seq in range(16):
                        sn, snk = S0n[seq % 2], ("S0n", seq % 2)
                        src4 = st_ssm.ap()[seq].rearrange("(q j) p n -> j p q n", j=2)
                        for j in range(2):
                            mk.dma("sp", sn[j * 64:(j + 1) * 64, :, :], src4[j], writes=[snk])
                        for half in range(2):
                            b = bank()
                            TR([(PS[:, b, i * 128:(i + 1) * 128], sn[:, half * 4 + i, :]) for i in range(4)], identf[:], [snk], [pk(b)])
                            CP(EV(), h0T[:, half * 4:half * 4 + 4, :], r3(PS[:, b, :]), [pk(b)], ["h0T"])
                        TT("dve", CTm, CT.rearrange("p g t -> p g t"), bc(seqcol[:, seq, :].unsqueeze(1), [128, 2, 128]), ALU.mult,
                           ["CT"], ["CTm"])
                        h0f = h0T.rearrange("p q x -> p (q x)")
                        for g in range(2):
                            MM([(PS[:, ob[g], :], CTm[:, g, :], h0f[:, g * 512:(g + 1) * 512], seq == 0, seq == 15)],
                               ["CTm", "h0T"], [pk(ob[g])])
                        TS("dve", xppm, xpp, seqrow[:, seq:seq + 1], ALU.mult, ["xpp"], ["xppm"])
                        dh = [bank(), bank()]
                        for q in range(8):
                            MM([(PS[:, dh[q // 4], (q % 4) * 128:(q % 4 + 1) * 128], xppm[:, q * 128:(q + 1) * 128],
                                 B_tm[:, 0, q // 4, :], True, True)], ["xppm", "B_tm"], [("dh", dh[q // 4]), pk(dh[q // 4])])
                        so, sok = Snew[seq % 2], ("Snew", seq % 2)
                        TT("dve", so, sn, bc(cdP[:, seq, :].unsqueeze(2), [128, 8, 128]), ALU.mult, [snk, "cdP"], [sok])
                        for half in range(2):
                            TT("dve", so[:, half * 4:half * 4 + 4, :], so[:, half * 4:half * 4 + 4, :], r3(PS[:, dh[half], :]), ALU.add,
                               [sok, ("dh", dh[half])], [sok, pk(dh[half])])
                        dst4 = o_ssm_s.ap()[seq].rearrange("(q j) p n -> j p q n", j=2)
                        for j in range(2):
                            mk.dma("pool", dst4[j], so[j * 64:(j + 1) * 64, :, :], reads=[sok], key=sok)
                for g in range(2):
                    yg = yv[:, g * 512:(g + 1) * 512]
                    TT("dve", h3(yg, 8), h3(PS[:, ob[g], :], 8), bc(ea[:, g * 8:g * 8 + 8].unsqueeze(2), [128, 8, 64]), ALU.mult,
                       [pk(ob[g]), "ea"], ["yv"])
                    TT("dve", yg, yg, PS[:, yb[g], :], ALU.add, ["yv", ("yb", g, 0), ("yb", g, 1)], ["yv"])
                release(ob); release(yb)
                TT("dve", h3(t2), h3(xs_tm[:, sub, :]), bc(r_dsk[:].unsqueeze(2), [128, 16, 64]), ALU.mult, ["xs_tm"], ["t2"])
                TT("dve", yv, yv, t2, ALU.add, ["yv", "t2"], ["yv"])
                TT("dve", yv, yv, zs[:, sub, :], ALU.mult, ["yv", "zs"], ["yv"])
                for g in range(2):
                    ACT(junk32[:, 0:512], yv[:, g * 512:(g + 1) * 512], AF.Square, ["yv"], ["junk32", "ss2"], accum=ss2[:, g:g + 1])
                ACT(rs2, ss2, AF.Ln, ["ss2"], ["rs2"], scale=1.0 / 512, bias=NORM_EPS)
                ACT(rs2, rs2, AF.Exp, ["rs2"], ["rs2"], scale=-0.5)
                for g in range(2):
                    TS("dve", yn[:, g * 512:(g + 1) * 512], yv[:, g * 512:(g + 1) * 512], rs2[:, g:g + 1], ALU.mult, ["yv", "rs2"], ["yn"])
                for half in range(2):
                    b = bank()
                    TR([(PS16[:, b, j * 128:(j + 1) * 128], yn[:, (half * 4 + j) * 128:(half * 4 + j + 1) * 128]) for j in range(4)],
                       identb[:], ["yn"], [pk(b)])
                    TT("dve", YT[:, half * 4:half * 4 + 4, tsl], r3(PS16[:, b, 0:512]),
                       bc(g_ssdn[:, half * 4:half * 4 + 4].unsqueeze(2), [128, 4, 128]), ALU.mult, [pk(b)], ["YT"])
                if K == "P":
                    for g in range(2):
                        b = bank()
                        MM([(PS[:, b, :], B_tm[:, sub, g, :], xpp[:, g * 512:(g + 1) * 512], True, True)], ["B_tm", "xpp"], [pk(b)])
                        hg_ = hT32[:, g * 512:(g + 1) * 512]
                        TT("dve", h3(hg_, 8), h3(hg_, 8), bc(cdd[:, g * 8:g * 8 + 8].unsqueeze(2), [128, 8, 64]), ALU.mult,
                           ["hT32", "cdd"], ["hT32"])
                        TT("dve", hg_, hg_, PS[:, b, :], ALU.add, ["hT32", pk(b)], ["hT32"])
                    CP("act", hTb[:], hT32[:], ["hT32"], ["hTb"])

            if cfg.get("stop") == 3:
                mk.barrier()
                continue
            mk.barrier()
            arena.off = mark_wkv
            NCB = 4 if K == "P" else 2
            NH = 2 * NCB
            NG = 8 // NCB
            ARt = A([NCB, nsub, 2, 128], BF16)
            KtT = A([NCB, NTK], BF16); BtT = A([NCB, NTK], BF16)
            V_tm = A([nsub, NH * 64], BF16); Kh_tm = A([nsub, NH * 64], BF16); Bh_tm = A([nsub, NH * 64], BF16)
            bvT = A([NCB, NTK], BF16)
            gamL = A([8, ncq])
            f32n = lambda: A([NTK])
            u_r, u_k, u_v = f32n(), f32n(), f32n()
            sw, asg, kk0, sq, kk, kmod, bvec, Gs, D1, GH = [f32n() for _ in range(10)]
            KhT, BhT, vb = A([NTK], BF16), A([NTK], BF16), A([NTK], BF16)
            off_AM = arena.off
            AM = A([NH, 512], BF16)
            Pb = [A([NH, 128], BF16) for _ in range(2)]
            Qb = [A([NH, 128], BF16) for _ in range(2)]
            Zb = A([NH, 128], BF16)
            Xs = A([NH * 64], BF16); Us = A([NH * 64], BF16)
            o_g = A([NH * 64]); onb = A([NH * 64], BF16); sqg = A([NH * 64])
            onf = sqg
            t1 = sqg.rearrange("p (c x) -> p c x", x=128)
            Stmp = A([NCB, 64])
            gs1, gs2, gmean, gm2, gvar, grstd = [A([NH]) for _ in range(6)]
            NB2 = NH * 128 // 512

            def bank2():
                if NB2 == 1:
                    return bank()
                bb = bank()
                while bb % 2:
                    bb = bank()
                rot[0] += 1
                return bb

            def pk2(b_):
                return [pk(b_)] if NB2 == 1 else [pk(b_), pk(b_ + 1)]

            def PSn(b_):
                return PS[:, b_:b_ + NB2, :].rearrange("p b (c x) -> p (b c) x", x=128)
            rm = rmask[kind]
            nit = 6 if K == "P" else 2
            if K == "S":
                SbS = A([16, 8, 64], BF16); S32S = A([16, 8, 64])
                ARm = A([2, 16, 2, 128], BF16)
                U_all = A([1024], BF16); V_all = A([1024], BF16); Kh_all = A([1024], BF16); Bh_all = A([1024], BF16)
                S0w = [A([8, 128]) for _ in range(2)]
                for seq in range(16):
                    sw_, swk = S0w[seq % 2], ("S0w", seq % 2)
                    srcw = st_wkv.ap()[seq].rearrange("(c j) v k -> j v c k", j=2)
                    for j in range(2):
                        mk.dma("sp", sw_[0:64, :, j * 64:(j + 1) * 64], srcw[j], writes=[swk])
                    b = bank()
                    TR([(PS[:, b, i * 64:(i + 1) * 64], sw_[0:64, i, :]) for i in range(8)], identf[0:64, 0:64], [swk], [pk(b)])
                    CP("dve", S32S[:, seq, :, :], PS[:, b, :].rearrange("p (c v) -> p c v", v=64), [pk(b)], ["S32S"])
                    CP("act", SbS[:, seq, :, :], S32S[:, seq, :, :], ["S32S"], ["SbS"])

            class _Stop(Exception):
                pass
            try:
              if cfg.get("stop") == 30:
                  raise _Stop()
              for hg in range(NG):
                  for cbl in range(NCB):
                      cb = NCB * hg + cbl
                      cs = slice(cb, cb + 1)
                      s = wslot()
                      for q in range(3):
                          wload(s, w_in, 0, 8, C_RW + q * 1024 + cb * 128, 128, col_off=q * 128)
                      for q, (uq, uk_) in enumerate(((u_r, "u_r"), (u_k, "u_k"), (u_v, "u_v"))):
                          b = bank()
                          MM([(PS[:, b, 0:NTK], WS[:, s, dk, q * 128:(q + 1) * 128], unT[:, dk, :], dk == 0, dk == 7) for dk in range(8)],
                             ["unT", ("ws", s)], [pk(b)])
                          tshift(b, q * 8 + cb, uq, uk_)
                      b = bank()
                      MM([(PS[:, b, 0:NTK], W2A[0:64, cb * 128:(cb + 1) * 128], LoT[0:64, :], True, True)], ["W2A", "LoT"], [pk(b)])
                      ACT(sw, PS[:, b, 0:NTK], AF.Sigmoid, [pk(b)], ["sw"], bias=c_w0[:, cs])
                      b = bank()
                      MM([(PS[:, b, 0:NTK], W2A[64:128, cb * 128:(cb + 1) * 128], LoT[64:128, :], True, True)], ["W2A", "LoT"], [pk(b)])
                      ACT(asg, PS[:, b, 0:NTK], AF.Sigmoid, [pk(b)], ["asg"], bias=c_a0[:, cs])
                      TS("dve", kk0, u_k, c_kk[:, cs], ALU.mult, ["u_k"], ["kk0"])
                      TT("dve", sq, kk0, kk0, ALU.mult, ["kk0"], ["sq"])
                      b = bank()
                      MM([(PS[:, b, 0:NTK], blkones[:], sq, True, True)], ["sq"], [pk(b)])
                      TS("dve", sq, PS[:, b, 0:NTK], 1e-24, ALU.max, [pk(b)], ["sq"])
                      ACT(sq, sq, AF.Ln, ["sq"], ["sq"])
                      ACT(sq, sq, AF.Exp, ["sq"], ["sq"], scale=-0.5)
                      TT("dve", kk, kk0, sq, ALU.mult, ["kk0", "sq"], ["kk"])
                      TS("dve", kmod, asg, c_ka[:, cs], ALU.mult, ["asg"], ["kmod"], s2=c_ka[:, cs], op1=ALU.subtract)
                      STT("dve", kmod, kmod, 1.0, u_k, ALU.add, ALU.mult, ["kmod", "u_k"], ["kmod"])
                      TT("dve", bvec, kk, asg, ALU.mult, ["kk", "asg"], ["bvec"])
                      SCAN(Gs, rm[:, 0:NTK], sw, ["sw"], ["Gs"])
                      TT("dve", D1, Gs, sw, ALU.subtract, ["Gs", "sw"], ["D1"])
                      G3 = Gs.rearrange("p (c l) -> p c l", l=Lq)
                      GH3 = GH.rearrange("p (c l) -> p c l", l=Lq)
                      TT("dve", GH3, bc(G3[:, :, Lq - 1:Lq], [128, ncq, Lq]), G3, ALU.subtract, ["Gs"], ["GH"])
                      eNG, eD1, eGH, eG = kk0, D1, GH, Gs
                      ACT(eNG, Gs, AF.Exp, ["Gs", "kk"], ["kk0"], scale=CEXP)
                      ACT(eD1, D1, AF.Exp, ["D1"], ["D1"], scale=-CEXP)
                      ACT(eGH, GH, AF.Exp, ["GH"], ["GH"], scale=-CEXP)
                      ACT(eG, Gs, AF.Exp, ["Gs", "kk0", "D1", "GH"], ["Gs"], scale=-CEXP)
                      CP("dve", gamL[:, cb, :].unsqueeze(2), G3[:, :, Lq - 1:Lq], ["Gs"], ["gamL"])
                      TT("dve", ARt[:, cbl, :, 1, :], r3(u_r), r3(eG), ALU.mult, ["u_r", "Gs"], ["ARt"])
                      STT("dve", ARt[:, cbl, :, 0, :], r3(kk), -1.0, r3(eD1), ALU.mult, ALU.mult, ["kk", "D1"], ["ARt"])
                      TT("dve", KtT[:, cbl, :], kmod, eNG, ALU.mult, ["kmod", "kk0"], ["KtT"])
                      TT("dve", BtT[:, cbl, :], bvec, eNG, ALU.mult, ["bvec", "kk0"], ["BtT"])
                      TT("dve", KhT, kmod, eGH, ALU.mult, ["kmod", "GH"], ["KhT"])
                      TT("dve", BhT, bvec, eGH, ALU.mult, ["bvec", "GH"], ["BhT"])
                      CP("act", vb, u_v, ["u_v"], ["vb"])
                      for (srcT, sk_, dst, dk_) in ((KhT, "KhT", Kh_tm, "Kh_tm"), (BhT, "BhT", Bh_tm, "Bh_tm"), (vb, "vb", V_tm, "V_tm")):
                          b2 = bank()
                          TR([(PS16[:, b2, i * 128:(i + 1) * 128], srcT[:, i * 128:(i + 1) * 128]) for i in range(nsub)],
                             identb[:], [sk_], [pk(b2)])
                          CP(EV(), dst[:, :, cbl * 128:(cbl + 1) * 128], r3(PS16[:, b2, 0:nsub * 128]), [pk(b2)], [dk_])
                      prod = sq
                      STT("dve", prod, u_r, c_rk[:, cs], kmod, ALU.mult, ALU.mult, ["u_r", "kmod", "sq"], ["sq"])
                      b = bank()
                      MM([(PS[:, b, 0:NTK], blkones[:], prod, True, True)], ["sq"], [pk(b)])
                      TT("dve", bvT[:, cbl, :], PS[:, b, 0:NTK], u_v, ALU.mult, [pk(b), "u_v"], ["bvT"])

                  if cfg.get("stop") == 305:
                      raise _Stop()
                  if K == "S":
                      for cbl in range(NCB):
                          TT("dve", ARm[:, cbl, :, :, :], bc(ARt[:, cbl, 0, :, :].unsqueeze(1), [128, 16, 2, 128]),
                             bc(seqcol[:, :, :].unsqueeze(2), [128, 16, 2, 128]), ALU.mult, ["ARt"], ["ARm"])
                  if cfg.get("stop") == 31:
                      raise _Stop()
                  heads = [(cbl, j) for cbl in range(NCB) for j in (0, 1)]
                  for sub in range(nsub):
                      tsl = slice(sub * 128, (sub + 1) * 128)
                      bN = bank2()
                      nitems = []
                      for i, (cbl, j) in enumerate(heads):
                          hs = slice(j * 64, (j + 1) * 64)
                          bA = bank()
                          ar = ARt[hs, cbl, sub, :, :].rearrange("p a t -> p (a t)")
                          MM([(PS[:, bA, 0:256], BtT[hs, cbl, tsl], ar, True, True),
                              (PS[:, bA, 256:512], KtT[hs, cbl, tsl], ar, True, True)], ["BtT", "KtT", "ARt"], [pk(bA)])
                          TT("dve", AM[:, i, :], PS[:, bA, :], mAM[K][:], ALU.mult, [pk(bA)], ["AM"])
                          nitems.append((PSn(bN)[:, i, :], ARt[hs, cbl, sub, 0, :], BtT[hs, cbl, tsl], True, True))
                      if cfg.get("stop") == 321:
                          raise _Stop()
                      MM(nitems, ["ARt", "BtT"], pk2(bN))
                      TT("dve", Pb[0], PSn(bN), bc(mGTb[K][:].unsqueeze(1), [128, NH, 128]), ALU.mult, pk2(bN), [("Pb", 0)])
                      TT("dve", Zb, AM[:, :, 0:128], bc(identb[:].unsqueeze(1), [128, NH, 128]), ALU.add, ["AM"], ["Zb"])
                      if cfg.get("stop") == 322:
                          raise _Stop()
                      Pc, Pk = Pb[0], ("Pb", 0)
                      Qc, Qk = AM[:, :, 0:128], "AM"
                      for it in range(nit):
                          bP = bank2()
                          MM([(PSn(bP)[:, i, :], Qc[:, i, :], Pc[:, i, :], True, True) for i in range(NH)], [Pk, Qk], pk2(bP))
                          Pn, Pnk = Pb[(it + 1) % 2], ("Pb", (it + 1) % 2)
                          CP("act", Pn, PSn(bP), pk2(bP), [Pnk])
                          if it < nit - 1:
                              bQ = bank2()
                              MM([(PSn(bQ)[:, i, :], Pc[:, i, :], Qc[:, i, :], True, True) for i in range(NH)],
                                 [Pk, Qk], pk2(bQ))
                              Qn, Qnk = Qb[it % 2], ("Qb", it % 2)
                              CP("act", Qn, PSn(bQ), pk2(bQ), [Qnk])
                          bZ = bank2()
                          MM([(PSn(bZ)[:, i, :], Pn[:, i, :], Zb[:, i, :], True, True) for i in range(NH)],
                             [Pnk, "Zb"], pk2(bZ))
                          TT("dve", Zb, Zb, PSn(bZ), ALU.add, ["Zb"] + pk2(bZ), ["Zb"])
                          Pc, Pk = Pn, Pnk
                          if it < nit - 1:
                              Qc, Qk = Qn, Qnk
                      if cfg.get("stop") == 32:
                          raise _Stop()
                      bX = bank()
                      items = []
                      for i, (cbl, j) in enumerate(heads):
                          cb = NCB * hg + cbl
                          hs = slice(j * 64, (j + 1) * 64)
                          hc = slice(i * 64, (i + 1) * 64)
                          o_ = PS[:, bX, i * 64:(i + 1) * 64]
                          if K == "P":
                              items.append((o_, ARt[:, cbl, sub, 0, :], Sb[:, cb, j, :], True, False))
                          else:
                              for sq_ in range(16):
                                  items.append((o_, ARm[hs, cbl, sq_, 0, :], SbS[hs, sq_, cb, :], sq_ == 0, False))
                          items.append((o_, AM[:, i, 256:384], V_tm[:, sub, hc], False, True))
                      MM(items, ["ARt", "Sb", "AM", "V_tm", "ARm", "SbS"], [pk(bX)])
                      CP("act", Xs, PS[:, bX, 0:NH * 64], [pk(bX)], ["Xs"])
                      bU = bank()
                      MM([(PS[:, bU, i * 64:(i + 1) * 64], Zb[:, i, :], Xs[:, i * 64:(i + 1) * 64], True, True) for i in range(NH)],
                         ["Zb", "Xs"], [pk(bU)])
                      CP("act", Us, PS[:, bU, 0:NH * 64], [pk(bU)], ["Us"])
                      bO = bank()
                      items = []
                      for i, (cbl, j) in enumerate(heads):
                          cb = NCB * hg + cbl
                          hs = slice(j * 64, (j + 1) * 64)
                          hc = slice(i * 64, (i + 1) * 64)
                          o_ = PS[:, bO, i * 64:(i + 1) * 64]
                          if K == "P":
                              items.append((o_, ARt[:, cbl, sub, 1, :], Sb[:, cb, j, :], True, False))
                          else:
                              for sq_ in range(16):
                                  items.append((o_, ARm[hs, cbl, sq_, 1, :], SbS[hs, sq_, cb, :], sq_ == 0, False))
                          items.append((o_, AM[:, i, 128:256], Us[:, hc], False, False))
                          items.append((o_, AM[:, i, 384:512], V_tm[:, sub, hc], False, True))
                      MM(items, ["ARt", "Sb", "AM", "V_tm", "Us", "ARm", "SbS"], [pk(bO)])
                      CP("dve", o_g, PS[:, bO, 0:NH * 64], [pk(bO)], ["o_g"])
                      if cfg.get("stop") == 33:
                          raise _Stop()
                      if K == "P":
                          bS = bank()
                          sitems = []
                          for cbl in range(NCB):
                              dst = PS[:, bS, cbl * 128:(cbl + 1) * 128]
                              cc = slice(cbl * 128, (cbl + 1) * 128)
                              sitems += [(dst, Bh_tm[:, sub, cc], Us[:, cc], True, False), (dst, Kh_tm[:, sub, cc], V_tm[:, sub, cc], False, True)]
                          MM(sitems, ["Bh_tm", "Kh_tm", "Us", "V_tm"], [pk(bS)])
                          Sg = S32[:, NCB * hg:NCB * hg + NCB, :]
                          TT("dve", Stmp, Sg, bc(gamL[:, NCB * hg:NCB * hg + NCB, sub:sub + 1], [128, NCB, 64]), ALU.mult, ["S32", "gamL"], ["Stmp"])
                          dsv = PS[:, bS, 0:NCB * 128].rearrange("p (c x) -> p c x", x=128)
                          for j in range(2):
                              hs = slice(j * 64, (j + 1) * 64)
                              TT("dve", S32[hs, NCB * hg:NCB * hg + NCB, :], Stmp[hs, :, :], dsv[hs, :, j * 64:(j + 1) * 64], ALU.add,
                                 ["Stmp", pk(bS)], ["S32"])
                          for j in range(2):
                              hs = slice(j * 64, (j + 1) * 64)
                              CP("act", Sb[hs, NCB * hg:NCB * hg + NCB, j, :], S32[hs, NCB * hg:NCB * hg + NCB, :], ["S32"], ["Sb"])
                      else:
                          gc = slice(hg * 256, (hg + 1) * 256)
                          CP("act", U_all[:, gc], Us, ["Us"], ["U_all"])
                          CP("act", V_all[:, gc], V_tm[:, 0, :], ["V_tm"], ["V_all"])
                          CP("dve", Kh_all[:, gc], Kh_tm[:, 0, :], ["Kh_tm"], ["Kh_all"])
                          CP("dve", Bh_all[:, gc], Bh_tm[:, 0, :], ["Bh_tm"], ["Bh_all"])
                      if cfg.get("stop") == 34:
                          raise _Stop()
                      RED("dve", gs1, h3(o_g, NH), ["o_g"], ["gs1"])
                      ACT(sqg, o_g, AF.Square, ["o_g"], ["sqg"])
                      RED("dve", gs2, h3(sqg, NH), ["sqg"], ["gs2"])
                      TS("dve", gmean, gs1, 1.0 / 64, ALU.mult, ["gs1"], ["gmean"])
                      TT("dve", gm2, gmean, gmean, ALU.mult, ["gmean"], ["gm2"])
                      STT("dve", gvar, gs2, 1.0 / 64, gm2, ALU.mult, ALU.subtract, ["gs2", "gm2"], ["gvar"])
                      ACT(grstd, gvar, AF.Ln, ["gvar"], ["grstd"], bias=GN_EPS)
                      ACT(grstd, grstd, AF.Exp, ["grstd"], ["grstd"], scale=-0.5)
                      TT("dve", h3(onf, NH), h3(o_g, NH), bc(gmean.unsqueeze(2), [128, NH, 64]), ALU.subtract, ["o_g", "gmean"], ["sqg"])
                      TT("dve", h3(onb, NH), h3(onf, NH), bc(grstd.unsqueeze(2), [128, NH, 64]), ALU.mult, ["sqg", "grstd"], ["onb"])
                      cbs = slice(NCB * hg, NCB * hg + NCB)
                      bT = bank()
                      TR([(PS16[:, bT, i * 128:(i + 1) * 128], onb[:, i * 128:(i + 1) * 128]) for i in range(NCB)], identb[:], ["onb"], [pk(bT)])
                      bG = bank()
                      MM([(PS[:, bG, i * 128:(i + 1) * 128], G2[:, (NCB * hg + i) * 128:(NCB * hg + i + 1) * 128], sgT[:, tsl], True, True)
                          for i in range(NCB)], ["G2", "sgT"], [pk(bG)])
                      TT("dve", t1, r3(PS16[:, bT, 0:NCB * 128]), bc(c_lnw[:, cbs].unsqueeze(2), [128, NCB, 128]), ALU.mult, [pk(bT)], ["sqg"])
                      TT("dve", t1, t1, bc(c_lnb[:, cbs].unsqueeze(2), [128, NCB, 128]), ALU.add, ["sqg"], ["sqg"])
                      TT("dve", t1, t1, bvT[:, :, tsl], ALU.add, ["sqg", "bvT"], ["sqg"])
                      TT("dve", YT[:, 8 + NCB * hg:8 + NCB * hg + NCB, tsl], t1, r3(PS[:, bG, 0:NCB * 128]), ALU.mult, ["sqg", pk(bG)], ["YT"])
            except _Stop:
                stopped = True
            else:
                stopped = False
            if K == "S" and not (stopped and cfg.get("stop", 0) < 40):
                mk.barrier()
                off_keep = arena.off
                arena.off = off_AM
                Bhm = A([1024], BF16); Khm = A([1024], BF16)
                Stm = A([8, 64]); Sn2 = A([8, 64])
                Sout = [A([8, 128]) for _ in range(2)]
                for seq in range(16):
                    TS("dve", Bhm, Bh_all, seqrow[:, seq:seq + 1], ALU.mult, ["Bh_all"], ["Bhm"])
                    TS("dve", Khm, Kh_all, seqrow[:, seq:seq + 1], ALU.mult, ["Kh_all"], ["Khm"])
                    ds = hold(2)
                    for cb in range(8):
                        dst = PS[:, ds[cb // 4], (cb % 4) * 128:(cb % 4 + 1) * 128]
                        cc = slice(cb * 128, (cb + 1) * 128)
                        MM([(dst, Bhm[:, cc], U_all[:, cc], True, False), (dst, Khm[:, cc], V_all[:, cc], False, True)],
                           ["Bhm", "Khm", "U_all", "V_all"], [pk(ds[cb // 4])])
                    TT("dve", Stm, S32S[:, seq, :, :], bc(gamL[:, :, seq:seq + 1], [128, 8, 64]), ALU.mult, ["S32S", "gamL"], ["Stm"])
                    dsv = PS[:, ds[0]:ds[0] + 2, :].rearrange("p b (c x) -> p (b c) x", x=128)
                    for j in range(2):
                        hs = slice(j * 64, (j + 1) * 64)
                        TT("dve", Sn2[hs, :, :], Stm[hs, :, :], dsv[hs, :, j * 64:(j + 1) * 64], ALU.add,
                           ["Stm", pk(ds[0]), pk(ds[1])], ["Sn2"])
                    release(ds)
                    so, sok = Sout[seq % 2], ("Sout", seq % 2)
                    for half in range(2):
                        b = bank()
                        TR([(PS[0:64, b, i * 128:(i + 1) * 128], Sn2[:, half * 4 + i, :]) for i in range(4)], identf[:], ["Sn2"], [pk(b)])
                        CP(EV(), so[0:64, half * 4:half * 4 + 4, :], r3(PS[0:64, b, :]), [pk(b)], [sok])
                    mk.dma("pool", o_wkv_s.ap()[seq].rearrange("(c j) v k -> v c j k", j=2),
                           so[0:64, :, :].rearrange("p c (j k) -> p c j k", j=2), reads=[sok], key=sok)
                mk.barrier()
                arena.off = mark_seg
                stc = A([1536]); sth = A([3328]); cn2 = A([48])
                for blk in range(12):
                    CP("dve", cn2.rearrange("p (s j) -> p s j", j=3), cnewS[:, blk, :, :], ["cnewS"], ["cn2"])
                    b = bank()
                    TR([(PS[0:48, b, 0:128], cn2)], identf[:], ["cn2"], [pk(b)])
                    CP(EV(), stc[0:48, blk * 128:(blk + 1) * 128], PS[0:48, b, 0:128], [pk(b)], ["stc"])
                mk.dma("sp", o_conv_s.ap(), stc[0:48, :], reads=["stc"], key="o_conv_s")
                for g0 in range(0, 26, 4):
                    n_ = min(4, 26 - g0)
                    b = bank()
                    TR([(PS[0:16, b, i * 128:(i + 1) * 128], shnewS[:, g0 + i, :]) for i in range(n_)], identf[:], ["shnewS"], [pk(b)])
                    CP(EV(), sth[0:16, g0 * 128:(g0 + n_) * 128], PS[0:16, b, 0:n_ * 128], [pk(b)], ["sth"])
                mk.dma("sp", o_shift_s.ap(), sth[0:16, :], reads=["sth"], key="o_shift_s")

            if cfg.get("stop") == 4 or stopped:
                mk.barrier()
                continue
            if K == "P" and is_last_p:
                mk.barrier()
                arena.off = mark_seg
                stg = A([8, 128])
                for half in range(2):
                    b = bank()
                    TR([(PS[:, b, i * 128:(i + 1) * 128], hT32[:, (half * 4 + i) * 128:(half * 4 + i + 1) * 128]) for i in range(4)],
                       identf[:], ["hT32"], [pk(b)])
                    CP(EV(), stg[:, half * 4:half * 4 + 4, :], r3(PS[:, b, :]), [pk(b)], ["stg"])
                mk.dma("sp", o_ssm_p.ap().rearrange("(b r) n -> r b n", r=128), stg, reads=["stg"], key="o_ssm_p")
                stw = A([8, 128])
                for half in range(2):
                    b = bank()
                    TR([(PS[0:64, b, i * 128:(i + 1) * 128], S32[:, half * 4 + i, :]) for i in range(4)], identf[:], ["S32"], [pk(b)])
                    CP(EV(), stw[0:64, half * 4:half * 4 + 4, :], r3(PS[0:64, b, :]), [pk(b)], ["stw"])
                mk.dma("sp", o_wkv_p.ap().rearrange("(c j) v k -> v c j k", j=2),
                       stw[0:64, :, :].rearrange("p c (j k) -> p c j k", j=2), reads=["stw"], key="o_wkv_p")
                stc = A([1536])
                for grp in range(3):
                    b = bank()
                    TR([(PS[0:3, b, i * 128:(i + 1) * 128], cvcarry[:, grp * 4 + i, :]) for i in range(4)], identf[:], ["cvcarry"], [pk(b)])
                    CP(EV(), stc[0:3, grp * 512:(grp + 1) * 512], PS[0:3, b, :], [pk(b)], ["stc"])
                mk.dma("sp", o_conv_p.ap(), stc[0:3, :], reads=["stc"], key="o_conv_p")
                sts = A([128])
                b = bank()
                TR([(PS[0:26, b, 0:128], shcarry[:, :])], identf[:], ["shcarry"], [pk(b)])
                CP(EV(), sts[0:26, :], PS[0:26, b, 0:128], [pk(b)], ["sts"])
                mk.dma("sp", o_shift_p.ap(), sts[0:26, :], reads=["sts"], key="o_shift_p")

            if cfg.get("stop") == 5:
                mk.barrier()
                continue
            mk.barrier()
            arena.off = mark_seg
            for half in range(2):
                s0, s1 = wslot(), wslot()
                wload(s0, w_out, 0, 8, half * 512, 512)
                wload(s1, w_out, 1024, 8, half * 512, 512)
                for sub in range(nsub):
                    tsl = slice(sub * 128, (sub + 1) * 128)
                    b = bank()
                    MM([(PS[:, b, :], YT[:, kc, tsl], WS[:, (s0 if kc < 8 else s1), kc % 8, :], kc == 0, kc == 15) for kc in range(16)],
                       ["YT", ("ws", s0), ("ws", s1)], [pk(b)])
                    xh = x_tm[:, sub, half * 512:(half + 1) * 512]
                    TT("dve", xh, xh, PS[:, b, :], ALU.add, [("x", sub), pk(b)], [("x", sub)])
            for sub in range(nsub):
                dbg_tap("hmid_%s_%d" % (segname, sub), x_tm[:, sub, :], [128, 1024], ("x", sub))
            for sub in range(nsub):
                norm_T(x_tm[:, sub, :], ("x", sub), g_ffn, unT, "unT", sub * 128, None)
            actT = A([22, NTK], BF16)
            sgf = [A([NTK]) for _ in range(2)]
            for f0 in range(0, 22, 4):
                nf_ = min(4, 22 - f0)
                sg_, su_ = wslot(), wslot()
                wload(sg_, w_gate, 0, 8, f0 * 128, nf_ * 128)
                wload(su_, w_up, 0, 8, f0 * 128, nf_ * 128)
                for fi in range(nf_):
                    fb = f0 + fi
                    bg = bank()
                    MM([(PS[:, bg, 0:NTK], WS[:, sg_, dk, fi * 128:(fi + 1) * 128], unT[:, dk, :], dk == 0, dk == 7) for dk in range(8)],
                       ["unT", ("ws", sg_)], [pk(bg)])
                    bu = bank()
                    MM([(PS[:, bu, 0:NTK], WS[:, su_, dk, fi * 128:(fi + 1) * 128], unT[:, dk, :], dk == 0, dk == 7) for dk in range(8)],
                       ["unT", ("ws", su_)], [pk(bu)])
                    sgb, sgk = sgf[fb % 2], ("sgf", fb % 2)
                    ACT(sgb, PS[:, bg, 0:NTK], AF.Silu, [pk(bg)], [sgk])
                    TT("dve", actT[:, fb, :], sgb, PS[:, bu, 0:NTK], ALU.mult, [sgk, pk(bu)], ["actT"])
            for half in range(2):
                sl3 = [wslot(), wslot(), wslot()]
                wload(sl3[0], w_down, 0, 8, half * 512, 512)
                wload(sl3[1], w_down, 1024, 8, half * 512, 512)
                wload(sl3[2], w_down, 2048, 6, half * 512, 512)
                for sub in range(nsub):
                    tsl = slice(sub * 128, (sub + 1) * 128)
                    b = bank()
                    MM([(PS[:, b, :], actT[:, fb, tsl], WS[:, sl3[fb // 8], fb % 8, :], fb == 0, fb == 21) for fb in range(22)],
                       ["actT"] + [("ws", q) for q in sl3], [pk(b)])
                    xh = x_tm[:, sub, half * 512:(half + 1) * 512]
                    TT("dve", xh, xh, PS[:, b, :], ALU.add, [("x", sub), pk(b)], [("x", sub)])
            pT = A([2, NTK], BF16)
            ptm = A([256]); ptb = A([256], BF16)
            for sub in range(nsub):
                norm_T(x_tm[:, sub, :], ("x", sub), g_ple, unT, "unT", sub * 128, None)
                mk.dma("sp", ptm, pcat.ap()[row0 + sub * 128:row0 + (sub + 1) * 128, :], writes=["ptm"])
                CP("dve", ptb, ptm, ["ptm"], ["ptb"])
                b = bank()
                TR([(PS16[:, b, i * 128:(i + 1) * 128], ptb[:, i * 128:(i + 1) * 128]) for i in range(2)], identb[:], ["ptb"], [pk(b)])
                CP(EV(), pT[:, :, sub * 128:(sub + 1) * 128], r3(PS16[:, b, 0:256]), [pk(b)], ["pT"])
            gt = A([512])
            for half in range(2):
                s = wslot()
                wload(s, w_plg, 0, 8, half * 512, 512)
                for sub in range(nsub):
                    tsl = slice(sub * 128, (sub + 1) * 128)
                    bg = bank()
                    MM([(PS[:, bg, :], unT[:, dk, tsl], WS[:, s, dk, :], dk == 0, dk == 7) for dk in range(8)], ["unT", ("ws", s)], [pk(bg)])
                    ACT(gt, PS[:, bg, :], AF.Sigmoid, [pk(bg)], ["gt"])
                    bp = bank()
                    MM([(PS[:, bp, :], pT[:, kc, tsl], WPP[:, kc, half * 512:(half + 1) * 512], kc == 0, kc == 1) for kc in range(2)],
                       ["pT", "WPP"], [pk(bp)])
                    TT("dve", gt, gt, PS[:, bp, :], ALU.mult, ["gt", pk(bp)], ["gt"])
                    xh = x_tm[:, sub, half * 512:(half + 1) * 512]
                    TT("dve", xh, xh, gt, ALU.add, [("x", sub), "gt"], [("x", sub)])
            yo = [A([1024]) for _ in range(2)]
            r_gfin = A([1024])
            mk.dma("sp", r_gfin, bass.AP(vec["norm_final"], 0, [[0, 128], [1, 1024]]), writes=["r_gfin"])
            for sub in range(nsub):
                ss, rs = small[:, 0:1], small[:, 1:2]
                ACT(arena_tmp["junk"], x_tm[:, sub, :], AF.Square, [("x", sub)], ["un", "ss"], accum=ss)
                ACT(rs, ss, AF.Ln, ["ss"], ["rs"], scale=1.0 / D, bias=NORM_EPS)
                ACT(rs, rs, AF.Exp, ["rs"], ["rs"], scale=-0.5)
                STT("dve", yo[sub % 2], x_tm[:, sub, :], rs, r_gfin, ALU.mult, ALU.mult, [("x", sub), "rs", "r_gfin"], [("yo", sub % 2)])
                mk.dma("sp", ycat.ap()[row0 + sub * 128:row0 + (sub + 1) * 128, :], yo[sub % 2], reads=[("yo", sub % 2)],
                       key=("yo", sub % 2))
            mk.barrier()

        mk.barrier()
        mk.emit_all()
    return nc, dbg_out


_VEC_NAMES = ["norm_mix", "norm_ffn", "norm_ple", "ssd_norm", "conv_b", "shift_mu", "w0", "a0", "k_k", "k_a", "r_k",
              "ln_x_w", "ln_x_b", "dt_bias", "a_log", "d_skip"]


def make_in_map(inp, c):
    f = lambda a: np.ascontiguousarray(a, dtype=np.float32)
    m = {}
    m["xcat"] = f(np.concatenate([inp["x_prompt"][c], inp["x_sample"][16 * c:16 * c + 16].reshape(128, D)], axis=0))
    m["pcat"] = f(np.concatenate([inp["p_prompt"][0, c], inp["p_sample"][0, 16 * c:16 * c + 16].reshape(128, 256)], axis=0))
    m["st_ssm"] = f(inp["state_ssm"][0, 16 * c:16 * c + 16])
    m["st_conv"] = f(inp["state_conv"][0, 16 * c:16 * c + 16].reshape(48, 1536))
    m["st_wkv"] = f(inp["state_wkv"][0, 16 * c:16 * c + 16])
    m["st_shift"] = f(inp["state_shift"][0, 16 * c:16 * c + 16].reshape(16, 3328))
    for k in ("w_in", "w_out", "w_gate", "w_up", "w_down", "w_ple_gate", "w_ple_proj", "w2", "a2", "g2", "conv_w"):
        m[k] = f(inp[k][0])
    for k in _VEC_NAMES:
        m[k] = f(inp[k][0].reshape(1, -1))
    m["norm_final"] = f(inp["norm_final"].reshape(1, -1))
    return m


_NC_CACHE = {}


def kernel(**inputs):
    if "nc" not in _NC_CACHE:
        _NC_CACHE["nc"] = build()[0]
    nc = _NC_CACHE["nc"]
    in_maps = [make_in_map(inputs, c) for c in range(NCORES)]
    res = run_bass_kernel_spmd(nc, in_maps, core_ids=list(range(NCORES)))
    R = res.results
    y_prompt = np.stack([R[c]["ycat"][:2048] for c in range(NCORES)], axis=0)
    y_sample = np.concatenate([R[c]["ycat"][2048:].reshape(16, 8, D) for c in range(NCORES)], axis=0)
    ssm_p = np.stack([R[c]["ssm_p"].reshape(16, 64, 128) for c in range(NCORES)], axis=0)[None]
    conv_p = np.stack([R[c]["conv_p"] for c in range(NCORES)], axis=0)[None]
    wkv_p = np.stack([R[c]["wkv_p"] for c in range(NCORES)], axis=0)[None]
    shift_p = np.stack([R[c]["shift_p"].reshape(1, 3328) for c in range(NCORES)], axis=0)[None]
    ssm_s = np.concatenate([R[c]["ssm_s"] for c in range(NCORES)], axis=0)[None]
    conv_s = np.concatenate([R[c]["conv_s"].reshape(16, 3, 1536) for c in range(NCORES)], axis=0)[None]
    wkv_s = np.concatenate([R[c]["wkv_s"] for c in range(NCORES)], axis=0)[None]
    shift_s = np.concatenate([R[c]["shift_s"].reshape(16, 1, 3328) for c in range(NCORES)], axis=0)[None]
    outs = (y_prompt, y_sample, ssm_p, conv_p, wkv_p, shift_p, ssm_s, conv_s, wkv_s, shift_s)
    return tuple(np.ascontiguousarray(o, dtype=np.float32) for o in outs)
```
